# Optimizing a Trainium2 kernel written in Bass

```python
import math
import jax
import jax.numpy as jnp
from jax import lax
import numpy as np

D_MODEL = 2048
BATCH = 4
SEQ = 2048
DEPTH = 4
DEC_BATCH = 2
DEC_SEQ = 8192
PAST_LEN = 128

N_MIXERS = 2
N_ATTN_LAYERS = (DEPTH + 1) // 2
N_SSD_LAYERS = DEPTH // 2

DILATED_CONFIGS = ((128, 1), (512, 4), (2048, 16))
N_GROUPS = len(DILATED_CONFIGS)
ATTN_HEADS = 16
HEAD_DIM = D_MODEL // ATTN_HEADS
QKV_DIM = N_GROUPS * 3 * ATTN_HEADS * HEAD_DIM

EXPAND = 2
D_INNER = EXPAND * D_MODEL
SSD_HEADDIM = 64
SSD_HEADS = D_INNER // SSD_HEADDIM
SSD_GROUPS = 8
D_STATE = 128
D_CONV = 5
CHUNK = 128
CONV_DIM = D_INNER + 2 * SSD_GROUPS * D_STATE
IN_PROJ_DIM = D_INNER + CONV_DIM + 2 * SSD_HEADS

D_FF = 4 * D_MODEL

DEEPNORM_ALPHA = (2.0 * DEPTH) ** 0.25
DEEPNORM_BETA = (8.0 * DEPTH) ** -0.25
LN_EPS = 1e-5
NEG_BIG = -1e30

kernel_name = "hybrid_dilated_attn_ssd_encoder"


def layer_norm(x, g, b):
    xf = x.astype(jnp.float32)
    mu = jnp.mean(xf, axis=-1, keepdims=True)
    var = jnp.mean(jnp.square(xf - mu), axis=-1, keepdims=True)
    return ((xf - mu) * lax.rsqrt(var + LN_EPS) * g + b).astype(x.dtype)


def alibi_slopes(n):
    return 2.0 ** (-8.0 * jnp.arange(1, n + 1, dtype=jnp.float32) / n)


def band_attention(q, k, v, dil, half, slopes):
    n, l, h, hd = q.shape
    nb = -(-l // half)
    lp = nb * half
    qb = jnp.pad(q, ((0, 0), (0, lp - l), (0, 0), (0, 0))).reshape(n, nb, half, h, hd)
    kvpad = ((0, 0), (half, lp - l + half), (0, 0), (0, 0))
    kb = jnp.pad(k, kvpad).reshape(n, nb + 2, half, h, hd)
    vb = jnp.pad(v, kvpad).reshape(n, nb + 2, half, h, hd)
    kw = jnp.concatenate([kb[:, :-2], kb[:, 1:-1], kb[:, 2:]], axis=2)
    vw = jnp.concatenate([vb[:, :-2], vb[:, 1:-1], vb[:, 2:]], axis=2)
    scores = jnp.einsum("nbqhd,nbkhd->nbhqk", qb, kw).astype(jnp.float32) * (hd ** -0.5)
    qpos = jnp.arange(nb)[:, None] * half + jnp.arange(half)[None, :]
    kpos = jnp.arange(nb)[:, None] * half - half + jnp.arange(3 * half)[None, :]
    rel = kpos[:, None, :] - qpos[:, :, None]
    valid = (jnp.abs(rel) <= half) & (kpos[:, None, :] >= 0) & (kpos[:, None, :] < l)
    dist = jnp.abs(rel).astype(jnp.float32) * dil
    bias = -slopes[None, :, None, None] * dist[:, None]
    scores = jnp.where(valid[:, None], scores + bias, NEG_BIG)
    m = jnp.max(scores, axis=-1, keepdims=True)
    p = jnp.exp(scores - m)
    den = jnp.sum(p, axis=-1, keepdims=True)
    out = jnp.einsum("nbhqk,nbkhd->nbqhd", (p / den).astype(v.dtype), vw)
    lse = (m + jnp.log(den))[..., 0]
    out = out.reshape(n, lp, h, hd)[:, :l]
    lse = lse.transpose(0, 1, 3, 2).reshape(n, lp, h)[:, :l]
    return out, lse


def dilated_group(q, k, v, window, dil, slopes):
    b, s, h, hd = q.shape
    l = s // dil
    half = window // (2 * dil)

    def to_sub(t):
        return t.reshape(b, l, dil, h, hd).transpose(0, 2, 1, 3, 4).reshape(b * dil, l, h, hd)

    out, lse = band_attention(to_sub(q), to_sub(k), to_sub(v), dil, half, slopes)
    out = out.reshape(b, dil, l, h, hd).transpose(0, 2, 1, 3, 4).reshape(b, s, h, hd)
    lse = lse.reshape(b, dil, l, h).transpose(0, 2, 1, 3).reshape(b, s, h)
    return out, lse


def attention_mixer(x, w_qkv, w_o):
    b, s, _ = x.shape
    qkv = (x @ w_qkv).reshape(b, s, N_GROUPS, 3, ATTN_HEADS, HEAD_DIM)
    slopes = alibi_slopes(N_GROUPS * ATTN_HEADS).reshape(N_GROUPS, ATTN_HEADS)
    outs, lses = [], []
    for g, (window, dil) in enumerate(DILATED_CONFIGS):
        o, lse = dilated_group(qkv[:, :, g, 0], qkv[:, :, g, 1], qkv[:, :, g, 2], window, dil, slopes[g])
        outs.append(o)
        lses.append(lse)
    w = jax.nn.softmax(jnp.stack(lses, axis=0), axis=0)
    o = jnp.einsum("gbsh,gbshd->bshd", w.astype(x.dtype), jnp.stack(outs, axis=0))
    return o.reshape(b, s, ATTN_HEADS * HEAD_DIM) @ w_o


def centred_depthwise_conv(u, w, bias):
    c = u.shape[-1]
    pad = D_CONV // 2
    out = lax.conv_general_dilated(u, w[:, None, :].astype(u.dtype), window_strides=(1,),
                                   padding=[(pad, pad)], dimension_numbers=("NWC", "WIO", "NWC"),
                                   feature_group_count=c)
    return out + bias


def ssd_scan(x, dt, a, bm, cm):
    b, s, h, p = x.shape
    g, n = bm.shape[2], bm.shape[3]
    e = h // g
    c = s // CHUNK
    l = CHUNK
    x = x.astype(jnp.float32)
    dt = dt.astype(jnp.float32)
    xc = (x * dt[..., None]).reshape(b, c, l, g, e, p)
    la = (dt * a.astype(jnp.float32)).reshape(b, c, l, g, e)
    a_cs = jnp.cumsum(la, axis=2)
    bc = bm.astype(jnp.float32).reshape(b, c, l, g, n)
    cc = cm.astype(jnp.float32).reshape(b, c, l, g, n)
    seg = a_cs[:, :, :, None] - a_cs[:, :, None, :]
    tri = jnp.tril(jnp.ones((l, l), dtype=bool))[None, None, :, :, None, None]
    lmat = jnp.exp(jnp.where(tri, seg, -jnp.inf))
    cb = jnp.einsum("bctgn,bcsgn->bctsg", cc, bc)
    y_diag = jnp.einsum("bctsg,bctsge,bcsgep->bctgep", cb, lmat, xc)
    decay_to_end = jnp.exp(a_cs[:, :, -1:] - a_cs)
    chunk_states = jnp.einsum("bclgn,bclge,bclgep->bcgepn", bc, decay_to_end, xc)
    chunk_decay = jnp.exp(a_cs[:, :, -1])

    def step(state, inp):
        cs, cd = inp
        return cd[..., None, None] * state + cs, state

    init = jnp.zeros((b, g, e, p, n), dtype=jnp.float32)
    _, prev = lax.scan(step, init, (jnp.moveaxis(chunk_states, 1, 0), jnp.moveaxis(chunk_decay, 1, 0)))
    prev = jnp.moveaxis(prev, 0, 1)
    y_off = jnp.einsum("bclgn,bcgepn,bclge->bclgep", cc, prev, jnp.exp(a_cs))
    return (y_diag + y_off).reshape(b, s, h, p)


def gated_group_rmsnorm(y, z, w):
    b, s, d = y.shape
    u = (y.astype(jnp.float32) * jax.nn.silu(z.astype(jnp.float32))).reshape(b, s, SSD_GROUPS, d // SSD_GROUPS)
    u = u * lax.rsqrt(jnp.mean(jnp.square(u), axis=-1, keepdims=True) + LN_EPS)
    return (u.reshape(b, s, d) * w).astype(z.dtype)


def ssd_mixer(x, w_in, conv_w, conv_b, dt_bias, a_log, d_skip, norm_w, w_out):
    b, s, _ = x.shape
    zxbcdt = x @ w_in
    z = zxbcdt[..., :D_INNER]
    xbc = zxbcdt[..., D_INNER:D_INNER + CONV_DIM]
    dt_raw = zxbcdt[..., D_INNER + CONV_DIM:]
    xbc = jax.nn.silu(centred_depthwise_conv(xbc, conv_w, conv_b))
    xs = xbc[..., :D_INNER].reshape(b, s, SSD_HEADS, SSD_HEADDIM)
    bm = xbc[..., D_INNER:D_INNER + SSD_GROUPS * D_STATE].reshape(b, s, SSD_GROUPS, D_STATE)
    cm = xbc[..., D_INNER + SSD_GROUPS * D_STATE:].reshape(b, s, SSD_GROUPS, D_STATE)
    dt = jax.nn.softplus(dt_raw.astype(jnp.float32).reshape(b, s, 2, SSD_HEADS) + dt_bias.astype(jnp.float32))
    a = -jnp.exp(a_log.astype(jnp.float32))
    y_fwd = ssd_scan(xs, dt[:, :, 0], a[0], bm, cm)
    y_bwd = ssd_scan(xs[:, ::-1], dt[:, ::-1, 1], a[1], bm[:, ::-1], cm[:, ::-1])[:, ::-1]
    y = y_fwd + y_bwd + d_skip.astype(jnp.float32)[:, None] * xs.astype(jnp.float32)
    y = gated_group_rmsnorm(y.reshape(b, s, D_INNER), z, norm_w)
    return y @ w_out


def squared_relu_mlp(x, w1, w2):
    h = jax.nn.relu(x @ w1)
    return (h * h) @ w2


def run_trunk(x, attn_w_qkv, attn_w_o, ssd_w_in, ssd_conv_w, ssd_conv_b, ssd_dt_bias, ssd_a_log,
              ssd_d, ssd_norm_w, ssd_w_out, mlp_w1, mlp_w2, ln_g, ln_b):
    for i in range(DEPTH):
        j = i // N_MIXERS
        if i % N_MIXERS == 0:
            mix = attention_mixer(x, attn_w_qkv[j], attn_w_o[j])
        else:
            mix = ssd_mixer(x, ssd_w_in[j], ssd_conv_w[j], ssd_conv_b[j], ssd_dt_bias[j], ssd_a_log[j],
                            ssd_d[j], ssd_norm_w[j], ssd_w_out[j])
        x = layer_norm(DEEPNORM_ALPHA * x + mix, ln_g[i, 0], ln_b[i, 0])
        x = layer_norm(DEEPNORM_ALPHA * x + squared_relu_mlp(x, mlp_w1[i], mlp_w2[i]), ln_g[i, 1], ln_b[i, 1])
    return x


def setup_inputs(seed: int = 0) -> dict:
    key = jax.random.key(seed)
    ks = jax.random.split(key, 20)
    f32 = jnp.float32
    nrm = lambda k, shape, scale: jax.random.normal(k, shape, dtype=f32) * scale
    dt0 = jnp.exp(jax.random.uniform(ks[7], (N_SSD_LAYERS, 2, SSD_HEADS), dtype=f32,
                                     minval=math.log(1e-3), maxval=math.log(1e-1)))
    return {
        "x_prompt": nrm(ks[0], (BATCH, SEQ, D_MODEL), 1.0),
        "x_sample": nrm(ks[1], (DEC_BATCH, DEC_SEQ, D_MODEL), 1.0),
        "attn_w_qkv": nrm(ks[2], (N_ATTN_LAYERS, D_MODEL, QKV_DIM), D_MODEL ** -0.5),
        "attn_w_o": nrm(ks[3], (N_ATTN_LAYERS, ATTN_HEADS * HEAD_DIM, D_MODEL), DEEPNORM_BETA * (ATTN_HEADS * HEAD_DIM) ** -0.5),
        "ssd_w_in": nrm(ks[4], (N_SSD_LAYERS, D_MODEL, IN_PROJ_DIM), D_MODEL ** -0.5),
        "ssd_conv_w": nrm(ks[5], (N_SSD_LAYERS, D_CONV, CONV_DIM), D_CONV ** -0.5),
        "ssd_conv_b": nrm(ks[6], (N_SSD_LAYERS, CONV_DIM), 0.01),
        "ssd_dt_bias": dt0 + jnp.log(-jnp.expm1(-dt0)),
        "ssd_a_log": jnp.log(jax.random.uniform(ks[8], (N_SSD_LAYERS, 2, SSD_HEADS), dtype=f32, minval=1.0, maxval=16.0)),
        "ssd_d": 1.0 + nrm(ks[9], (N_SSD_LAYERS, SSD_HEADS), 0.1),
        "ssd_norm_w": 1.0 + nrm(ks[10], (N_SSD_LAYERS, D_INNER), 0.02),
        "ssd_w_out": nrm(ks[11], (N_SSD_LAYERS, D_INNER, D_MODEL), DEEPNORM_BETA * D_INNER ** -0.5),
        "mlp_w1": nrm(ks[12], (DEPTH, D_MODEL, D_FF), D_MODEL ** -0.5),
        "mlp_w2": nrm(ks[13], (DEPTH, D_FF, D_MODEL), DEEPNORM_BETA * D_FF ** -0.5),
        "ln_g": 1.0 + nrm(ks[14], (DEPTH, 2, D_MODEL), 0.02),
        "ln_b": nrm(ks[15], (DEPTH, 2, D_MODEL), 0.02),
    }


def reference(x_prompt, x_sample, attn_w_qkv, attn_w_o, ssd_w_in, ssd_conv_w, ssd_conv_b, ssd_dt_bias,
              ssd_a_log, ssd_d, ssd_norm_w, ssd_w_out, mlp_w1, mlp_w2, ln_g, ln_b):
    y_prompt = run_trunk(x_prompt, attn_w_qkv, attn_w_o, ssd_w_in, ssd_conv_w, ssd_conv_b, ssd_dt_bias,
                         ssd_a_log, ssd_d, ssd_norm_w, ssd_w_out, mlp_w1, mlp_w2, ln_g, ln_b)
    y_sample = run_trunk(x_sample, attn_w_qkv, attn_w_o, ssd_w_in, ssd_conv_w, ssd_conv_b, ssd_dt_bias,
                         ssd_a_log, ssd_d, ssd_norm_w, ssd_w_out, mlp_w1, mlp_w2, ln_g, ln_b)
    return (y_prompt, y_sample)
```

```python
import math
import sys
from contextlib import ExitStack
import numpy as np
import concourse.bass as bass
import concourse.mybir as mybir
from concourse.bass_utils import run_bass_kernel_spmd

F32 = mybir.dt.float32
BF16 = mybir.dt.bfloat16
ALU = mybir.AluOpType
AF = mybir.ActivationFunctionType

D = 2048
KC = 16
NGRP = 3
DILS = (1, 4, 16)
NH = 16
HD = 128
QKVD = 9 * D
DIN = 4096
CONVD = 6144
INPD = 10368
SH = 64
SP = 64
SG = 8
SN = 128
DFF = 8192
LN_EPS = 1e-5
def bcl(ap, n):
    sh = list(ap.shape)
    return ap.rearrange("p (h o) -> p h o", o=1).broadcast_to([sh[0], sh[1], n])


ENGS = ("sync", "act", "pool", "dve", "pe")
NEGM = -30000.0
ABIG = 1.0e6


class Ph:
    def __init__(self, nc, name, gb=None):
        self.nc = nc
        self.name = name
        self.ops = {e: [] for e in ENGS}
        self.gb = gb
        self.semcnt = gb.gsemcnt
        self.base = dict(gb.gsemcnt)
        self.dmap = {}
        self.barrier = list(gb.barrier)
        self.st = ExitStack()
        self.nbuf = 0
        self.psums = []

    def sb(self, shape, dt):
        self.nbuf += 1
        return self.st.enter_context(self.nc.sbuf_tensor(f"{self.name}_b{self.nbuf}", list(shape), dt))

    def ps(self, shape, dt=F32):
        self.nbuf += 1
        t = self.st.enter_context(self.nc.psum_tensor(f"{self.name}_p{self.nbuf}", list(shape), dt))
        self.psums.append((t, dt))
        return t

    def op(self, eng, fn, waits=(), sig=None, dma=False):
        ev = None
        inc = 16 if dma else 1
        if sig is True:
            sig = eng
        if sig is not None and sig not in ENGS:
            if sig not in self.dmap:
                self.dmap[sig] = f"d{len(self.dmap)}"
            sig = self.dmap[sig]
        if sig is not None:
            c = self.semcnt.get(sig, 0) + inc
            self.semcnt[sig] = c
            ev = (sig, c)
        ws = list(self.barrier)
        for w in waits:
            if w is None:
                continue
            if isinstance(w, list):
                ws.extend([x for x in w if x is not None])
            else:
                ws.append(w)
        self.ops[eng].append((fn, ws, ev, inc, sys._getframe(1).f_lineno))
        return ev

    def check_deadlock(self):
        cnt = dict(self.base)
        pos = {e: 0 for e in ENGS}
        progress = True
        while progress:
            progress = False
            for e in ENGS:
                while pos[e] < len(self.ops[e]):
                    fn, ws, ev, inc, ln = self.ops[e][pos[e]]
                    if all(cnt.get(s_, 0) >= c_ for (s_, c_) in ws):
                        if ev is not None:
                            cnt[ev[0]] = cnt.get(ev[0], 0) + inc
                        pos[e] += 1
                        progress = True
                    else:
                        break
        stuck = {e: pos[e] for e in ENGS if pos[e] < len(self.ops[e])}
        if stuck:
            msg = []
            for e, p in stuck.items():
                fn, ws, ev, inc, ln = self.ops[e][p]
                bad = [(s_, c_, cnt.get(s_, 0)) for (s_, c_) in ws if cnt.get(s_, 0) < c_]
                msg.append(f"{e}@{p}/{len(self.ops[e])} line {ln} waits {bad}")
            raise RuntimeError(f"DEADLOCK in phase {self.name}: " + "; ".join(msg))

    def emit(self):
        nc = self.nc
        gb = self.gb
        bt = gb.bar_f
        e_d = self.op("dve", lambda e: e.memset(bt[:, 0:1], 0.0), sig=True)
        e_a = self.op("act", lambda e: e.copy(bt[:, 1:2], bt[:, 2:3]), sig=True)
        e_p = self.op("pool", lambda e: e.memset(bt[:, 3:4], 0.0), sig=True)
        evs = [e_d, e_a, e_p]
        if self.psums:
            pt, pdt = self.psums[0]
            if pdt == F32:
                e_t = self.op("pe", lambda e: e.matmul(pt[:, 0:8], gb.bar_b[:, 0:128], gb.bar_b[:, 128:136], start=True, stop=True),
                              waits=evs, sig=True)
            else:
                e_t = self.op("pe", lambda e: e.transpose(pt[:, 0:128], gb.bar_b[:, 0:128], gb.bar_b[:, 0:128]), waits=evs, sig=True)
            evs.append(e_t)
        e_s = self.op("sync", lambda e: e.dma_start(out=gb.bar_d[1:2, :], in_=gb.bar_d[0:1, :]), waits=evs, sig="bsync", dma=True)
        evs.append(e_s)
        gb.barrier = evs
        self.check_deadlock()
        if True:
            for n in self.semcnt:
                if n not in gb.gsems:
                    gb.gsems[n] = gb.gst.enter_context(nc.semaphore(f"g_{n}"))
            sems = gb.gsems
            block = gb.block

            def mk(e):
                def body(eng):
                    waited = gb.waited[e]
                    for fn, waits, ev, inc, _ln in self.ops[e]:
                        for (s, c) in waits:
                            if waited.get(s, 0) < c:
                                eng.wait_ge(sems[s], c)
                                waited[s] = c
                        if fn is None:
                            continue
                        ins = fn(eng)
                        if ev is not None:
                            ins.then_inc(sems[ev[0]], inc)
                return body

            block.sync(mk("sync"))
            block.scalar(mk("act"))
            block.gpsimd(mk("pool"))
            block.vector(mk("dve"))
            block.tensor(mk("pe"))
        self.st.close()


class Ring:
    def __init__(self, bufs):
        self.bufs = bufs
        self.free = [None] * len(bufs)
        self.i = 0

    def get(self):
        k = self.i % len(self.bufs)
        self.i += 1
        return k, self.bufs[k], self.free[k]

    def release(self, k, ev):
        self.free[k] = ev


def host_consts():
    c = {}
    c["ident"] = np.eye(128, dtype=np.float32)
    k = np.arange(128)[:, None]
    t = np.arange(128)[None, :]
    tri_f = (k <= t).astype(np.float32)
    tri_b = (k >= t).astype(np.float32)
    c["tri"] = np.stack([tri_f, tri_b], 1).reshape(128, 256)
    c["ones"] = np.ones((128, 128), np.float32)
    nm = np.stack([np.where(k <= t, 0.0, NEGM), np.where(k >= t, 0.0, NEGM)], 0).astype(np.float32)
    c["negm"] = np.concatenate([np.tile(nm[0], (1, 4)), np.tile(nm[1], (1, 4))], 1)
    sel = np.zeros((64, 64, 128), np.float32)
    for h in range(64):
        sel[h, h, :] = 1.0
    c["sel"] = sel.reshape(64, 64 * 128)
    kk = np.arange(128)[:, None].astype(np.float64)
    qq = np.arange(128)[None, :].astype(np.float64)
    relA = 64 + kk - qq
    relB = kk - 64 - qq
    vA = kk <= qq
    vB = kk >= qq

    def mk(vA_, vB_):
        a = np.where(vA_, np.abs(relA), ABIG)
        b = np.where(vB_, np.abs(relB), ABIG)
        return np.concatenate([a, b], 1).astype(np.float32)

    inter = mk(vA, vB)
    first = mk(vA & False, vB & (kk >= 64))
    last = mk(vA & (kk < 64), vB & False)
    both = mk(vA & False, vB & False)
    c["absrel"] = np.concatenate([inter, first, last, both], 1)
    return c


class Builder:
    def __init__(self, units, depth, stage_limit=None):
        self.units = units
        self.T = sum(units)
        self.depth = depth
        self.uoff = [sum(units[:i]) for i in range(len(units))]
        self.NT = self.T // 512
        self.stage_limit = stage_limit
        self.nc = bass.Bass("TRN2", target_bir_lowering=False)
        self.pi = 0
        self.gsemcnt = {}
        self.gsems = {}
        self.barrier = []
        self.waited = {e: {} for e in ENGS}
        self.gst = ExitStack()

    def din(self, name, shape, dt=F32):
        return self.nc.dram_tensor(name, list(shape), dt, kind="ExternalInput").ap()

    def dscr(self, name, shape, dt):
        if name in getattr(self, "ext_scratch", ()):
            return self.nc.dram_tensor(name, list(shape), dt, kind="ExternalInput").ap()
        return self.nc.dram_tensor(name, list(shape), dt).ap()

    def ph(self, name):
        self.pi += 1
        return Ph(self.nc, f"p{self.pi}{name}", self)

    def build(self):
        nc = self.nc
        T = self.T
        na, ns = (self.depth + 1) // 2, self.depth // 2
        self.x_in = self.din("x_in", [T, D])
        self.w_qkv = self.din("attn_w_qkv", [na, D, QKVD])
        self.w_o = self.din("attn_w_o", [na, D, D])
        self.w_in = self.din("ssd_w_in", [max(ns, 1), D, INPD])
        self.w_out = self.din("ssd_w_out", [max(ns, 1), DIN, D])
        self.w1 = self.din("mlp_w1", [self.depth, D, DFF])
        self.w2 = self.din("mlp_w2", [self.depth, DFF, D])
        self.lng = self.din("ln_g", [self.depth * 2, D])
        self.lnb = self.din("ln_b", [self.depth * 2, D])
        self.convw = self.din("convw", [max(ns, 1), 128, 48 * 5])
        self.convb = self.din("convb", [max(ns, 1), 128, 48])
        self.dtb = self.din("dtb", [max(ns, 1), 128, 128])
        self.alog = self.din("alog", [max(ns, 1), 128, 128])
        self.dsk = self.din("dsk", [max(ns, 1), 128, 64])
        self.normw = self.din("normw", [max(ns, 1), 128, 32])
        self.c_ident = self.din("c_ident", [128, 128])
        self.c_tri = self.din("c_tri", [128, 256])
        self.c_ones = self.din("c_ones", [128, 128])
        self.c_negm = self.din("c_negm", [128, 1024])
        self.c_sel = self.din("c_sel", [64, 64 * 128])
        self.c_absrel = self.din("c_absrel", [128, 1024])
        self.y_out = nc.dram_tensor("y_out", [T, D], F32, kind="ExternalOutput").ap()
        self.xres = self.dscr("xres", [T, D], F32)
        self.xT = self.dscr("xT", [self.NT, 128, KC, 512], BF16)
        self.qkvg = [self.dscr(f"qkv{g}", [T, 3 * D], BF16) for g in range(NGRP)]
        self.og = [self.dscr(f"og{g}", [T, NH * 129], F32) for g in range(NGRP)]
        self.zz = self.dscr("zz", [T, DIN], F32)
        self.dtr = self.dscr("dtr", [T, 128], F32)
        self.xbcT = [self.dscr(f"xbcT{u}", [48, 128, S + 4], BF16) for u, S in enumerate(self.units)]
        self.xs = self.dscr("xs", [T, DIN], BF16)
        self.btm = self.dscr("btm", [T, 1024], BF16)
        self.bct = self.dscr("bct", [16, 128, T], BF16)
        self.yfb = [self.dscr(f"yfb{d}", [T, DIN], F32) for d in range(2)]

        stages = []
        stages.append(("init", lambda: self.phase_init()))
        for i in range(self.depth):
            j = i // 2
            if i % 2 == 0:
                stages.append((f"qkv{i}", lambda j=j: self.phase_qkv(j)))
                stages.append((f"att{i}", lambda j=j: self.phase_att(j)))
                stages.append((f"oproj{i}", lambda i=i, j=j: self.phase_proj_ln("oproj", i, j)))
            else:
                stages.append((f"inproj{i}", lambda j=j: self.phase_inproj(j)))
                stages.append((f"conv{i}", lambda j=j: self.phase_conv(j)))
                stages.append((f"scan{i}", lambda j=j: self.phase_scan(j)))
                stages.append((f"gate{i}", lambda i=i, j=j: self.phase_proj_ln("gate", i, j)))
            stages.append((f"mlp{i}", lambda i=i: self.phase_proj_ln("mlp", i, i)))
        if self.stage_limit is not None:
            stages = stages[: self.stage_limit]
        only = getattr(self, "only", None)
        import os as _os
        if _os.environ.get("ONLY_STAGES"):
            only = tuple(_os.environ["ONLY_STAGES"].split(","))
        if only is not None:
            stages = [st_ for st_ in stages if st_[0] in only]
        self.bar_d = self.dscr("bar_d", [2, 16], F32)
        self.bar_f = self.gst.enter_context(nc.sbuf_tensor("bar_f", [128, 8], F32))
        self.bar_b = self.gst.enter_context(nc.sbuf_tensor("bar_b", [128, 136], BF16))
        self.block = self.gst.enter_context(nc.Block())
        ph0 = self.ph("zero")
        ph0.op("pool", lambda e: e.memset(self.bar_f[:], 0.0), sig=True)
        ph0.op("pool", lambda e: e.memset(self.bar_b[:], 0.0), sig=True)
        ph0.emit()
        for name, fn in stages:
            fn()
        self.phase_final()
        self.gst.close()
        return nc

    def load_const(self, ph, src, shape, dt, eng="pool"):
        buf = ph.sb(shape, dt)
        ev = ph.op(eng, lambda e: e.dma_start(out=buf[:], in_=src), sig=f"c{ph.nbuf}", dma=True)
        return buf, ev

    def phase_init(self):
        ph = self.ph("init")
        T = self.T
        ident, ev_id = self.load_const(ph, self.c_ident[:, :], [128, 128], BF16)
        xin = Ring([ph.sb([128, D], F32) for _ in range(2)])
        xb = Ring([ph.sb([128, D], BF16) for _ in range(2)])
        stg = Ring([ph.sb([128, KC, 512], BF16) for _ in range(2)])
        pst = Ring([ph.ps([128, D], BF16) for _ in range(2)])
        ev_copy = ph.op("sync", lambda e: e.dma_start(out=self.xres[:, :], in_=self.x_in[:, :]), sig="cp", dma=True)
        last = [ev_copy]
        for nt in range(self.NT):
            ks, sbuf, sfree = stg.get()
            evs = []
            for sub in range(4):
                t0 = nt * 512 + sub * 128
                k1, b1, f1 = xin.get()
                ev_ld = ph.op("sync", lambda e, b1=b1, t0=t0: e.dma_start(out=b1[:], in_=self.x_in[t0:t0 + 128, :]),
                              waits=[f1], sig=f"ld{k1}", dma=True)
                k2, b2, f2 = xb.get()
                ev_cv = ph.op("act", lambda e, b1=b1, b2=b2: e.copy(b2[:], b1[:]), waits=[ev_ld, f2], sig=True)
                xin.release(k1, ev_cv)
                ev_t = self.emit_transposes(ph, b2, KC, ident, ev_id, pst, sbuf, sub, [ev_cv, sfree])
                xb.release(k2, ev_t["pe"])
                evs.append(ev_t["evac"])
            ev_st = ph.op("sync", lambda e, sbuf=sbuf, nt=nt: e.dma_start(out=self.xT[nt], in_=sbuf[:]),
                          waits=evs, sig=f"st{ks}", dma=True)
            stg.release(ks, ev_st)
            last.append(ev_st)
        ph.op("sync", None, waits=last)
        ph.emit()

    def emit_transposes(self, ph, src_bf, nchunk, ident, ev_id, pst, dst, sub, waits, scale=None, scale_ev=None):
        res = {}
        for c0 in range(0, nchunk, 16):
            kp, pt, pf = pst.get()
            ev_pe = None
            for c in range(c0, min(c0 + 16, nchunk)):
                ev_pe = ph.op("pe", lambda e, pt=pt, c=c, c0=c0: e.transpose(
                    pt[:, (c - c0) * 128:(c - c0 + 1) * 128], src_bf[:, c * 128:(c + 1) * 128], ident[:]),
                    waits=[ev_id, pf] + list(waits), sig=True)
            n = min(16, nchunk - c0)
            if scale is None:
                ev_ev = ph.op("dve", lambda e, pt=pt, c0=c0, n=n: e.tensor_copy(
                    dst[:, c0:c0 + n, sub * 128:(sub + 1) * 128],
                    pt[:, 0:n * 128].rearrange("p (c t) -> p c t", t=128)),
                    waits=[ev_pe], sig=True)
            else:
                for c in range(c0, c0 + n):
                    ev_ev = ph.op("dve", lambda e, pt=pt, c0=c0, c=c: e.tensor_scalar(
                        dst[:, c, sub * 128:(sub + 1) * 128], pt[:, (c - c0) * 128:(c - c0 + 1) * 128],
                        scale[:, c:c + 1], None, ALU.mult), waits=[ev_pe, scale_ev], sig=True)
            pst.release(kp, ev_ev)
            res["pe"] = ev_pe
            res["evac"] = ev_ev
        return res

    def phase_qkv(self, j):
        ph = self.ph("qkv")
        W = self.w_qkv[j]
        xt = Ring([ph.sb([128, KC, 512], BF16) for _ in range(2)])
        wsl = Ring([ph.sb([128, KC, 512], BF16) for _ in range(2)])
        stg = Ring([ph.sb([128, 4, 512], BF16) for _ in range(3)])
        pss = Ring([ph.ps([128, 512], F32) for _ in range(4)])
        last = []
        ecnt = 0
        for nt in range(self.NT):
            kx, xb, xf = xt.get()
            ev_x = ph.op("sync", lambda e, xb=xb, nt=nt: e.dma_start(out=xb[:], in_=self.xT[nt]),
                         waits=[xf], sig=f"x{kx}", dma=True)
            ev_lastmm = None
            for nsl in range(QKVD // 512):
                kw, wb, wf = wsl.get()
                ev_w = ph.op("pool", lambda e, wb=wb, nsl=nsl: e.dma_start(
                    out=wb[:], in_=W[:, nsl * 512:(nsl + 1) * 512].rearrange("(c p) n -> p c n", p=128)),
                    waits=[wf], sig=f"w{kw}", dma=True)
                ksg, sg, sgf = stg.get()
                evs = []
                for sub in range(4):
                    kp, pt, pf = pss.get()
                    for c in range(KC):
                        ev_mm = ph.op("pe", lambda e, pt=pt, xb=xb, wb=wb, c=c, sub=sub: e.matmul(
                            pt[:], xb[:, c, sub * 128:(sub + 1) * 128], wb[:, c, :], start=(c == 0), stop=(c == KC - 1)),
                            waits=[ev_x, ev_w, pf], sig=(True if c == KC - 1 else None))
                    ecnt += 1
                    if ecnt % 2 == 0:
                        ev_e = ph.op("act", lambda e, pt=pt, sg=sg, sub=sub: e.copy(sg[:, sub, :], pt[:]),
                                     waits=[ev_mm, sgf], sig=True)
                    else:
                        ev_e = ph.op("dve", lambda e, pt=pt, sg=sg, sub=sub: e.tensor_copy(sg[:, sub, :], pt[:]),
                                     waits=[ev_mm, sgf], sig=True)
                    pss.release(kp, ev_e)
                    evs.append(ev_e)
                    ev_lastmm = ev_mm
                wsl.release(kw, ev_lastmm)
                ev_st = ph.op("sync", lambda e, sg=sg, nt=nt, nsl=nsl: e.dma_start(
                    out=self.qkvg[nsl // 12][nt * 512:(nt + 1) * 512, (nsl % 12) * 512:(nsl % 12 + 1) * 512].rearrange("(s p) n -> p s n", p=128),
                    in_=sg[:]), waits=evs, sig=f"s{ksg}", dma=True)
                stg.release(ksg, ev_st)
                last.append(ev_st)
            xt.release(kx, ev_lastmm)
        ph.op("sync", None, waits=last[-3:])
        ph.emit()

    def phase_att(self, j):
        ph = self.ph("att")
        scale = HD ** -0.5
        ident, ev_id = self.load_const(ph, self.c_ident[:, :], [128, 128], BF16)
        absrel, ev_ar = self.load_const(ph, self.c_absrel[:, :], [128, 1024], F32, eng="sync")
        kbuf = Ring([ph.sb([128, D], BF16) for _ in range(2)])
        qbuf = Ring([ph.sb([128, D], BF16) for _ in range(2)])
        vbuf = Ring([ph.sb([128, NH, 129], BF16) for _ in range(2)])
        kT = Ring([ph.sb([128, NH, 128], BF16) for _ in range(2)])
        qT = Ring([ph.sb([128, NH, 128], BF16) for _ in range(3)])
        acc = Ring([ph.sb([128, NH, 129], F32) for _ in range(3)])
        tbuf = Ring([ph.sb([128, 256], F32) for _ in range(3)])
        pbuf = Ring([ph.sb([128, 256], BF16) for _ in range(3)])
        ps_tr = Ring([ph.ps([128, D], BF16) for _ in range(2)])
        ps_s = Ring([ph.ps([128, 512], F32) for _ in range(2)])
        ps_o = Ring([ph.ps([128, 512], F32) for _ in range(2)])
        vinit = []
        for b in vbuf.bufs:
            vinit.append(ph.op("pool", lambda e, b=b: e.memset(b[:], 1.0), sig=True))
        kinit = []
        for b in kbuf.bufs:
            kinit.append(ph.op("pool", lambda e, b=b: e.memset(b[:], 0.0), sig=True))
        last = []
        cnt = 0
        for u, S in enumerate(self.units):
            for g, dil in enumerate(DILS):
                l = S // dil
                ntile = l // 128
                cq, ck, cv = 0, D, 2 * D
                for r in range(dil):
                    base = self.uoff[u] + r
                    prev_q = None
                    for jk in range(ntile + 1):
                        i0 = 128 * jk - 64
                        lo, hi = max(i0, 0), min(i0 + 128, l)
                        p0 = lo - i0
                        nrow = hi - lo
                        half = nrow < 128
                        rows = slice(base + dil * lo, base + dil * (hi - 1) + 1, dil)
                        kk_, kb, kf = kbuf.get()
                        kv_, vb, vf = vbuf.get()
                        wz = []
                        if half:
                            z0 = 0 if p0 == 64 else 64
                            wz.append(ph.op("pool", lambda e, kb=kb, z0=z0: e.memset(kb[z0:z0 + 64, :], 0.0),
                                            waits=[kf] + kinit, sig=True))
                            wz.append(ph.op("pool", lambda e, vb=vb, z0=z0: e.memset(vb[z0:z0 + 64, :, 0:128], 0.0),
                                            waits=[vf] + vinit, sig=True))
                        ev_k = ph.op("sync", lambda e, kb=kb, rows=rows, p0=p0, nrow=nrow, ck=ck, g=g: e.dma_start(
                            out=kb[p0:p0 + nrow, :], in_=self.qkvg[g][rows, ck:ck + D]),
                            waits=[kf] + kinit + wz, sig=f"k{kk_}", dma=True)
                        ev_v = ph.op("sync", lambda e, vb=vb, rows=rows, p0=p0, nrow=nrow, cv=cv, g=g: e.dma_start(
                            out=vb[p0:p0 + nrow, :, 0:128],
                            in_=self.qkvg[g][rows, cv:cv + D].rearrange("t (h d) -> t h d", d=128)),
                            waits=[vf] + vinit + wz, sig=f"v{kv_}", dma=True)
                        kkt, ktb, ktf = kT.get()
                        ev_kt = self.emit_tr16(ph, kb, ktb, ident, ev_id, ps_tr, [ev_k, ktf])
                        kbuf.release(kk_, ev_kt["pe"])
                        cur_q = None
                        if jk < ntile:
                            qrows = slice(base + dil * 128 * jk, base + dil * (128 * jk + 127) + 1, dil)
                            kq_, qb, qf = qbuf.get()
                            ev_q = ph.op("sync", lambda e, qb=qb, qrows=qrows, cq=cq, g=g: e.dma_start(
                                out=qb[:], in_=self.qkvg[g][qrows, cq:cq + D]), waits=[qf], sig=f"q{kq_}", dma=True)
                            kqt, qtb, qtf = qT.get()
                            ev_qt = self.emit_tr16(ph, qb, qtb, ident, ev_id, ps_tr, [ev_q, qtf])
                            qbuf.release(kq_, ev_qt["pe"])
                            ka, ab, af = acc.get()
                            cur_q = dict(qtb=qtb, kqt=kqt, ev=ev_qt["evac"], ab=ab, ka=ka, af=af, qrows=qrows, uses=[], accev=[])
                        if jk == 0:
                            mtype = 1
                        elif jk == ntile:
                            mtype = 2
                        else:
                            mtype = 0
                        if ntile == 1 and False:
                            pass
                        parts = []
                        if prev_q is not None:
                            parts.append((0, prev_q, False))
                        if cur_q is not None:
                            parts.append((1, cur_q, True))
                        ev_pv_last = None
                        for h in range(NH):
                            slope = 2.0 ** (-8.0 * (g * NH + h + 1) / (NGRP * NH))
                            cgh = -slope * dil / scale
                            kps, pst_, psf = ps_s.get()
                            ev_s = None
                            for (col, qd, first) in parts:
                                ev_s = ph.op("pe", lambda e, pst_=pst_, ktb=ktb, qd=qd, h=h, col=col: e.matmul(
                                    pst_[:, col * 128:(col + 1) * 128], ktb[:, h, :], qd["qtb"][:, h, :], start=True, stop=True),
                                    waits=[ev_kt["evac"], qd["ev"], psf], sig=True)
                            c0 = parts[0][0] * 128
                            c1 = (parts[-1][0] + 1) * 128
                            kt_, tb, tf = tbuf.get()
                            ev_t = ph.op("dve", lambda e, tb=tb, pst_=pst_, cgh=cgh, mtype=mtype, c0=c0, c1=c1: e.scalar_tensor_tensor(
                                tb[:, c0:c1], absrel[:, mtype * 256 + c0:mtype * 256 + c1], cgh, pst_[:, c0:c1],
                                ALU.mult, ALU.add), waits=[ev_s, ev_ar, tf], sig=True)
                            ps_s.release(kps, ev_t)
                            kp_, pb, pf_ = pbuf.get()
                            ev_p = ph.op("act", lambda e, pb=pb, tb=tb, c0=c0, c1=c1: e.activation(
                                pb[:, c0:c1], tb[:, c0:c1], AF.Exp, scale=scale), waits=[ev_t, pf_], sig=True)
                            tbuf.release(kt_, ev_p)
                            kpo, pso, pof = ps_o.get()
                            ev_pv = None
                            for (col, qd, first) in parts:
                                ev_pv = ph.op("pe", lambda e, pso=pso, pb=pb, vb=vb, h=h, col=col: e.matmul(
                                    pso[:, col * 129:(col + 1) * 129], pb[:, col * 128:(col + 1) * 128], vb[:, h, :],
                                    start=True, stop=True), waits=[ev_p, ev_v, pof], sig=True)
                            pbuf.release(kp_, ev_pv)
                            ev_pv_last = ev_pv
                            ev_a = None
                            for (col, qd, first) in parts:
                                if first:
                                    ev_a = ph.op("dve", lambda e, qd=qd, pso=pso, h=h, col=col: e.tensor_copy(
                                        qd["ab"][:, h, :], pso[:, col * 129:(col + 1) * 129]),
                                        waits=[ev_pv, qd["af"]], sig=True)
                                else:
                                    ev_a = ph.op("dve", lambda e, qd=qd, pso=pso, h=h, col=col: e.tensor_tensor(
                                        qd["ab"][:, h, :], qd["ab"][:, h, :], pso[:, col * 129:(col + 1) * 129], ALU.add),
                                        waits=[ev_pv] + qd["accev"][h:h + 1], sig=True)
                                if first:
                                    qd["accev"].append(ev_a)
                                else:
                                    qd["accev"][h] = ev_a
                            ps_o.release(kpo, ev_a)
                        kT.release(kkt, ev_pv_last)
                        vbuf.release(kv_, ev_pv_last)
                        if prev_q is not None:
                            qd = prev_q
                            ev_st = ph.op("sync", lambda e, qd=qd, g=g: e.dma_start(
                                out=self.og[g][qd["qrows"], :], in_=qd["ab"][:].rearrange("p h d -> p (h d)")),
                                waits=qd["accev"], sig=f"o{qd['ka']}", dma=True)
                            acc.release(qd["ka"], ev_st)
                            qT.release(qd["kqt"], ev_pv_last)
                            last.append(ev_st)
                        prev_q = cur_q
        ph.op("sync", None, waits=last[-3:])
        ph.emit()

    def emit_tr16(self, ph, src, dst, ident, ev_id, ps_tr, waits):
        kp, pt, pf = ps_tr.get()
        ev_pe = None
        for c in range(NH):
            ev_pe = ph.op("pe", lambda e, pt=pt, c=c: e.transpose(pt[:, c * 128:(c + 1) * 128], src[:, c * 128:(c + 1) * 128], ident[:]),
                          waits=[ev_id, pf] + list(waits), sig=True)
        ev_ev = ph.op("act", lambda e, pt=pt: e.copy(dst[:].rearrange("p h t -> p (h t)"), pt[:]), waits=[ev_pe], sig=True)
        ps_tr.release(kp, ev_ev)
        return {"pe": ev_pe, "evac": ev_ev}

    def phase_proj_ln(self, kind, i, j):
        ph = self.ph(kind)
        alpha = (2.0 * self.depth_full) ** 0.25
        lnidx = 2 * i + (1 if kind == "mlp" else 0)
        if kind == "oproj":
            KCX, W2 = 16, self.w_o[j]
        elif kind == "gate":
            KCX, W2 = 32, self.w_out[j]
        else:
            KCX, W2 = 64, self.w2[i]
        ident, ev_id = self.load_const(ph, self.c_ident[:, :], [128, 128], BF16)
        gb = ph.sb([128, D], F32)
        bb = ph.sb([128, D], F32)
        ev_g = ph.op("sync", lambda e: e.dma_start(out=gb[:], in_=self.lng[lnidx:lnidx + 1, :].broadcast_to([128, D])), sig="cg", dma=True)
        ev_b = ph.op("sync", lambda e: e.dma_start(out=bb[:], in_=self.lnb[lnidx:lnidx + 1, :].broadcast_to([128, D])), sig="cb", dma=True)
        actT = Ring([ph.sb([128, KCX, 512], BF16) for _ in range(1)])
        w2s = Ring([ph.sb([128, 16, 512], BF16) for _ in range(2)])
        xold = Ring([ph.sb([128, 4, D], F32) for _ in range(1)])
        xnb = Ring([ph.sb([128, D], BF16) for _ in range(2)])
        stg = Ring([ph.sb([128, KC, 512], BF16) for _ in range(1)])
        stat = ph.sb([128, 4, 6], F32)
        mv = ph.sb([128, 2], F32)
        rstd = ph.sb([128, 1], F32)
        epsb = ph.sb([128, 1], F32)
        ev_eps = ph.op("pool", lambda e: e.memset(epsb[:], LN_EPS), sig=True)
        ps_acc = [ph.ps([128, 512], F32) for _ in range(4)]
        ps_accf = [None] * 4
        ps_tr = Ring([ph.ps([128, D], BF16) for _ in range(1)])
        ps_a = Ring([ph.ps([128, 512], F32) for _ in range(2)])
        if kind == "mlp":
            xt = Ring([ph.sb([128, KC, 512], BF16) for _ in range(1)])
            w1s = Ring([ph.sb([128, KC, 256], BF16) for _ in range(2)])
            rl = Ring([ph.sb([128, 512], F32) for _ in range(2)])
        elif kind == "oproj":
            ogb = [Ring([ph.sb([128, NH, 129], F32) for _ in range(1)]) for _ in range(3)]
            rec = ph.sb([128, NH], F32)
            ob = Ring([ph.sb([128, D], BF16) for _ in range(2)])
        else:
            yfb = Ring([ph.sb([128, DIN], F32) for _ in range(1)])
            ybb = Ring([ph.sb([128, DIN], F32) for _ in range(1)])
            zzb = Ring([ph.sb([128, DIN], F32) for _ in range(1)])
            xsb = Ring([ph.sb([128, DIN], BF16) for _ in range(1)])
            ub = Ring([ph.sb([128, DIN], BF16) for _ in range(1)])
            junk = ph.sb([128, 512], F32)
            ss = ph.sb([128, 8], F32)
            dskb, ev_dsk = self.load_const(ph, self.dsk[j], [128, 64], F32, eng="sync")
            nwb, ev_nw = self.load_const(ph, self.normw[j], [128, 32], F32, eng="sync")
        last = []
        serial = [None]
        for nt in range(self.NT):
            t512 = nt * 512
            ka, aT, aTf = actT.get()
            kxo, xo, xof = xold.get()
            ev_xo = ph.op("sync", lambda e, xo=xo, t512=t512: e.dma_start(
                out=xo[:], in_=self.xres[t512:t512 + 512, :].rearrange("(s p) n -> p s n", p=128)),
                waits=[xof], sig="xo", dma=True)
            aT_ready = []
            if kind == "mlp":
                kx, xb, xf = xt.get()
                ev_x = ph.op("sync", lambda e, xb=xb, nt=nt: e.dma_start(out=xb[:], in_=self.xT[nt]), waits=[xf], sig="x", dma=True)
                ev_mm = None
                for sl in range(DFF // 256):
                    kw, wb, wf = w1s.get()
                    ev_w = ph.op("pool", lambda e, wb=wb, sl=sl: e.dma_start(
                        out=wb[:], in_=self.w1[i][:, sl * 256:(sl + 1) * 256].rearrange("(c p) n -> p c n", p=128)),
                        waits=[wf], sig=f"w1{kw}", dma=True)
                    for cc in range(2):
                        fc = sl * 2 + cc
                        kp, pt, pf = ps_a.get()
                        for c in range(KC):
                            ev_mm = ph.op("pe", lambda e, pt=pt, wb=wb, xb=xb, c=c, cc=cc: e.matmul(
                                pt[:], wb[:, c, cc * 128:(cc + 1) * 128], xb[:, c, :], start=(c == 0), stop=(c == KC - 1)),
                                waits=[ev_w, ev_x, pf], sig=(True if c == KC - 1 else None))
                        kr, rb, rf = rl.get()
                        ev_r = ph.op("act", lambda e, rb=rb, pt=pt: e.activation(rb[:], pt[:], AF.Relu), waits=[ev_mm, rf], sig=True)
                        ps_a.release(kp, ev_r)
                        ev_h = ph.op("dve", lambda e, aT=aT, rb=rb, fc=fc: e.tensor_tensor(aT[:, fc, :], rb[:], rb[:], ALU.mult),
                                     waits=[ev_r, aTf], sig=True)
                        rl.release(kr, ev_h)
                        aT_ready.append(ev_h)
                    w1s.release(kw, ev_mm)
                xt.release(kx, ev_mm)
            elif kind == "oproj":
                for sub in range(4):
                    t0 = t512 + sub * 128
                    bufs = []
                    evl = []
                    for g in range(3):
                        kg, gbuf, gf = ogb[g].get()
                        evl.append(ph.op("sync", lambda e, gbuf=gbuf, g=g, t0=t0: e.dma_start(
                            out=gbuf[:].rearrange("p h d -> p (h d)"), in_=self.og[g][t0:t0 + 128, :]),
                            waits=[gf], sig=f"og{g}", dma=True))
                        bufs.append((kg, gbuf))
                    b0, b1, b2 = bufs[0][1], bufs[1][1], bufs[2][1]
                    e1 = ph.op("dve", lambda e, b0=b0, b1=b1: e.tensor_tensor(b0[:], b0[:], b1[:], ALU.add), waits=evl[:2], sig=True)
                    e2 = ph.op("dve", lambda e, b0=b0, b2=b2: e.tensor_tensor(b0[:], b0[:], b2[:], ALU.add), waits=[e1, evl[2]], sig=True)
                    e3 = ph.op("dve", lambda e, b0=b0: e.reciprocal(rec[:], b0[:, :, 128]), waits=[e2, serial[0]], sig=True)
                    ko, obuf, of_ = ob.get()
                    e4 = ph.op("dve", lambda e, b0=b0, obuf=obuf: e.tensor_tensor(
                        obuf[:].rearrange("p (h d) -> p h d", d=128), b0[:, :, 0:128],
                        bcl(rec[:], 128), ALU.mult), waits=[e3, of_], sig=True)
                    serial[0] = e4
                    ogb[0].release(bufs[0][0], e4)
                    ogb[1].release(bufs[1][0], e1)
                    ogb[2].release(bufs[2][0], e2)
                    ev_t = self.emit_transposes(ph, obuf, 16, ident, ev_id, ps_tr, aT, sub, [e4, aTf])
                    ob.release(ko, ev_t["pe"])
                    aT_ready.append(ev_t["evac"])
            else:
                for sub in range(4):
                    t0 = t512 + sub * 128
                    k1, yf_, f1 = yfb.get()
                    k2, yb_, f2 = ybb.get()
                    k3, zz_, f3 = zzb.get()
                    k4, xs_, f4 = xsb.get()
                    l1 = ph.op("sync", lambda e, yf_=yf_, t0=t0: e.dma_start(out=yf_[:], in_=self.yfb[0][t0:t0 + 128, :]), waits=[f1], sig="l1", dma=True)
                    l2 = ph.op("sync", lambda e, yb_=yb_, t0=t0: e.dma_start(out=yb_[:], in_=self.yfb[1][t0:t0 + 128, :]), waits=[f2], sig="l2", dma=True)
                    l3 = ph.op("sync", lambda e, zz_=zz_, t0=t0: e.dma_start(out=zz_[:], in_=self.zz[t0:t0 + 128, :]), waits=[f3], sig="l3", dma=True)
                    l4 = ph.op("sync", lambda e, xs_=xs_, t0=t0: e.dma_start(out=xs_[:], in_=self.xs[t0:t0 + 128, :]), waits=[f4], sig="l4", dma=True)
                    e1 = ph.op("dve", lambda e, yf_=yf_, yb_=yb_: e.tensor_tensor(yf_[:], yf_[:], yb_[:], ALU.add), waits=[l1, l2], sig=True)
                    e2 = ph.op("dve", lambda e, yb_=yb_, xs_=xs_: e.tensor_tensor(
                        yb_[:].rearrange("p (h q) -> p h q", q=64), xs_[:].rearrange("p (h q) -> p h q", q=64),
                        bcl(dskb[:], 64), ALU.mult), waits=[e1, l4, ev_dsk], sig=True)
                    e3 = ph.op("dve", lambda e, yf_=yf_, yb_=yb_: e.tensor_tensor(yf_[:], yf_[:], yb_[:], ALU.add), waits=[e2], sig=True)
                    e4 = ph.op("act", lambda e, zz_=zz_: e.activation(zz_[:], zz_[:], AF.Silu), waits=[l3], sig=True)
                    e5 = ph.op("dve", lambda e, yf_=yf_, zz_=zz_: e.tensor_tensor(yf_[:], yf_[:], zz_[:], ALU.mult), waits=[e3, e4], sig=True)
                    e6 = None
                    for gg in range(8):
                        e6 = ph.op("act", lambda e, yf_=yf_, gg=gg: e.activation(
                            junk[:], yf_[:, gg * 512:(gg + 1) * 512], AF.Square, accum_out=ss[:, gg:gg + 1]),
                            waits=[e5, serial[0]], sig=True)
                    e7 = ph.op("dve", lambda e: e.tensor_scalar(ss[:], ss[:], 1.0 / 512, LN_EPS, ALU.mult, ALU.add), waits=[e6], sig=True)
                    e8a = ph.op("act", lambda e: e.activation(ss[:], ss[:], AF.Ln), waits=[e7], sig=True)
                    e8 = ph.op("act", lambda e: e.activation(ss[:], ss[:], AF.Exp, scale=-0.5), waits=[e8a], sig=True)
                    ku, ub_, uf = ub.get()
                    e9 = ph.op("dve", lambda e, yf_=yf_, ub_=ub_: e.tensor_tensor(
                        ub_[:].rearrange("p (g q) -> p g q", q=512), yf_[:].rearrange("p (g q) -> p g q", q=512),
                        bcl(ss[:], 512), ALU.mult), waits=[e8, uf], sig=True)
                    serial[0] = e9
                    e10 = e9
                    yfb.release(k1, e10)
                    ybb.release(k2, e3)
                    zzb.release(k3, e5)
                    xsb.release(k4, e2)
                    ev_t = self.emit_transposes(ph, ub_, 32, ident, ev_id, ps_tr, aT, sub, [e10, aTf], scale=nwb, scale_ev=ev_nw)
                    ub.release(ku, ev_t["pe"])
                    aT_ready.append(ev_t["evac"])
            ks_, sbuf, sfree = stg.get()
            tr_evs = []
            zevs = {}
            zbufs = {}
            nq = KCX // 16
            ev_mm = None
            for nsl in range(4):
                for q in range(nq):
                    kw, wb, wf = w2s.get()
                    ev_w = ph.op("pool", lambda e, wb=wb, q=q, nsl=nsl: e.dma_start(
                        out=wb[:], in_=W2[q * 2048:(q + 1) * 2048, nsl * 512:(nsl + 1) * 512].rearrange("(c p) n -> p c n", p=128)),
                        waits=[wf], sig=f"w2{kw}", dma=True)
                    for sub in range(4):
                        for c in range(16):
                            fst = (q == 0 and c == 0)
                            lst = (q == nq - 1 and c == 15)
                            ev_mm = ph.op("pe", lambda e, sub=sub, aT=aT, wb=wb, q=q, c=c, fst=fst, lst=lst: e.matmul(
                                ps_acc[sub][:], aT[:, q * 16 + c, sub * 128:(sub + 1) * 128], wb[:, c, :], start=fst, stop=lst),
                                waits=[ev_w, ps_accf[sub]] + (aT_ready if (nsl == 0 and q == 0 and sub == 0 and c == 0) else []),
                                sig=(True if lst else None))
                        if q == nq - 1:
                            zbufs[sub] = (0, xo[:, sub, :], None)
                            ev_z = ph.op("dve", lambda e, xo=xo, sub=sub, nsl=nsl: e.scalar_tensor_tensor(
                                xo[:, sub, nsl * 512:(nsl + 1) * 512], xo[:, sub, nsl * 512:(nsl + 1) * 512], alpha, ps_acc[sub][:],
                                ALU.mult, ALU.add), waits=[ev_mm, ev_xo], sig=True)
                            ps_accf[sub] = ev_z
                            zevs.setdefault(sub, []).append(ev_z)
                    if q < nq - 1:
                        kpd, ptd, pfd = ps_a.get()
                        ev_mk = ph.op("pe", lambda e, ptd=ptd: e.matmul(ptd[:, 0:8], ident[:], ident[:, 0:8], start=True, stop=True),
                                      waits=[pfd, ev_id], sig=True)
                        ps_a.release(kpd, ev_mk)
                        w2s.release(kw, ev_mk)
                    else:
                        w2s.release(kw, ev_mm)
            actT.release(ka, ev_mm)
            xo_done = []
            for sub in range(4):
                kz, zbuf, zf = zbufs[sub]
                t0 = t512 + sub * 128
                e_s = None
                for cch in range(4):
                    e_s = ph.op("dve", lambda e, zbuf=zbuf, cch=cch: e.bn_stats(stat[:, cch, :], zbuf[:, cch * 512:(cch + 1) * 512]),
                                waits=zevs[sub] + [serial[0]], sig=True)
                e_a = ph.op("dve", lambda e: e.bn_aggr(mv[:], stat[:].rearrange("p a b -> p (a b)")), waits=[e_s], sig=True)
                e_r0 = ph.op("act", lambda e: e.activation(rstd[:], mv[:, 1:2], AF.Ln, bias=epsb[:, 0:1]), waits=[e_a, ev_eps], sig=True)
                e_r = ph.op("act", lambda e: e.activation(rstd[:], rstd[:], AF.Exp, scale=-0.5), waits=[e_r0], sig=True)
                e_n = ph.op("dve", lambda e, zbuf=zbuf: e.tensor_scalar(zbuf, zbuf, mv[:, 0:1], rstd[:, 0:1], ALU.subtract, ALU.mult),
                            waits=[e_r], sig=True)
                serial[0] = e_n
                e_g = ph.op("pool", lambda e, zbuf=zbuf: e.tensor_tensor(zbuf, zbuf, gb[:], ALU.mult), waits=[e_n, ev_g], sig=True)
                e_b = ph.op("pool", lambda e, zbuf=zbuf: e.tensor_tensor(zbuf, zbuf, bb[:], ALU.add), waits=[e_g, ev_b], sig=True)
                ev_st = ph.op("sync", lambda e, zbuf=zbuf, t0=t0: e.dma_start(out=self.xres[t0:t0 + 128, :], in_=zbuf),
                              waits=[e_b], sig=f"sx{sub}", dma=True)
                last.append(ev_st)
                kn, nb_, nf = xnb.get()
                e_c = ph.op("act", lambda e, nb_=nb_, zbuf=zbuf: e.copy(nb_[:], zbuf), waits=[e_b, nf], sig=True)
                xo_done += [ev_st, e_c]
                ev_t = self.emit_transposes(ph, nb_, KC, ident, ev_id, ps_tr, sbuf, sub, [e_c, sfree])
                xnb.release(kn, ev_t["pe"])
                tr_evs.append(ev_t["evac"])
            xold.release(kxo, xo_done)
            ev_st = ph.op("sync", lambda e, sbuf=sbuf, nt=nt: e.dma_start(out=self.xT[nt], in_=sbuf[:]), waits=tr_evs, sig="sT", dma=True)
            stg.release(ks_, ev_st)
            last.append(ev_st)
        ph.op("sync", None, waits=last[-6:])
        ph.emit()

    def phase_inproj(self, j):
        ph = self.ph("inproj")
        W = self.w_in[j]
        xt = Ring([ph.sb([128, KC, 512], BF16) for _ in range(2)])
        wsl = Ring([ph.sb([128, KC, 512], BF16) for _ in range(2)])
        stgz = Ring([ph.sb([128, 4, 512], F32) for _ in range(2)])
        stgx = Ring([ph.sb([128, 4, 512], BF16) for _ in range(2)])
        pss = Ring([ph.ps([128, 512], F32) for _ in range(4)])
        last = []
        tile_unit = []
        for u, S in enumerate(self.units):
            for k in range(S // 512):
                tile_unit.append((u, k))
        for nt in range(self.NT):
            u, kt = tile_unit[nt]
            kx, xb, xf = xt.get()
            ev_x = ph.op("sync", lambda e, xb=xb, nt=nt: e.dma_start(out=xb[:], in_=self.xT[nt]), waits=[xf], sig=f"x{kx}", dma=True)
            ev_mm = None
            slabs = [(s * 512, 512, "z") for s in range(8)] + [(DIN + CONVD, 128, "dt")]
            for (c0, wdt, kind) in slabs:
                kw, wb, wf = wsl.get()
                ev_w = ph.op("pool", lambda e, wb=wb, c0=c0, wdt=wdt: e.dma_start(
                    out=wb[:, :, 0:wdt], in_=W[:, c0:c0 + wdt].rearrange("(c p) n -> p c n", p=128)), waits=[wf], sig=f"w{kw}", dma=True)
                ksg, sg, sgf = stgz.get()
                evs = []
                for sub in range(4):
                    kp, pt, pf = pss.get()
                    for c in range(KC):
                        ev_mm = ph.op("pe", lambda e, pt=pt, xb=xb, wb=wb, c=c, sub=sub, wdt=wdt: e.matmul(
                            pt[:, 0:wdt], xb[:, c, sub * 128:(sub + 1) * 128], wb[:, c, 0:wdt], start=(c == 0), stop=(c == KC - 1)),
                            waits=[ev_x, ev_w, pf], sig=(True if c == KC - 1 else None))
                    ev_e = ph.op("act", lambda e, pt=pt, sg=sg, sub=sub, wdt=wdt: e.copy(sg[:, sub, 0:wdt], pt[:, 0:wdt]),
                                 waits=[ev_mm, sgf], sig=True)
                    pss.release(kp, ev_e)
                    evs.append(ev_e)
                wsl.release(kw, ev_mm)
                if kind == "z":
                    dst = self.zz[nt * 512:(nt + 1) * 512, c0:c0 + 512].rearrange("(s p) n -> p s n", p=128)
                else:
                    dst = self.dtr[nt * 512:(nt + 1) * 512, :].rearrange("(s p) n -> p s n", p=128)
                ev_st = ph.op("sync", lambda e, sg=sg, dst=dst, wdt=wdt: e.dma_start(out=dst, in_=sg[:, :, 0:wdt]),
                              waits=evs, sig=f"sz{ksg}", dma=True)
                stgz.release(ksg, ev_st)
                last.append(ev_st)
            for s in range(12):
                c0 = DIN + s * 512
                kw, wb, wf = wsl.get()
                ev_w = ph.op("pool", lambda e, wb=wb, c0=c0: e.dma_start(
                    out=wb[:], in_=W[:, c0:c0 + 512].rearrange("(c p) n -> p c n", p=128)), waits=[wf], sig=f"w{kw}", dma=True)
                ksg, sg, sgf = stgx.get()
                evs = []
                for cc in range(4):
                    kp, pt, pf = pss.get()
                    for c in range(KC):
                        ev_mm = ph.op("pe", lambda e, pt=pt, xb=xb, wb=wb, c=c, cc=cc: e.matmul(
                            pt[:], wb[:, c, cc * 128:(cc + 1) * 128], xb[:, c, :], start=(c == 0), stop=(c == KC - 1)),
                            waits=[ev_x, ev_w, pf], sig=(True if c == KC - 1 else None))
                    ev_e = ph.op("dve", lambda e, pt=pt, sg=sg, cc=cc: e.tensor_copy(sg[:, cc, :], pt[:]), waits=[ev_mm, sgf], sig=True)
                    pss.release(kp, ev_e)
                    evs.append(ev_e)
                wsl.release(kw, ev_mm)
                ev_st = ph.op("sync", lambda e, sg=sg, u=u, s=s, kt=kt: e.dma_start(
                    out=self.xbcT[u][s * 4:(s + 1) * 4, :, 2 + kt * 512:2 + (kt + 1) * 512].rearrange("c p t -> p c t"), in_=sg[:]),
                    waits=evs, sig=f"sx{ksg}", dma=True)
                stgx.release(ksg, ev_st)
                last.append(ev_st)
            xt.release(kx, ev_mm)
        ph.op("sync", None, waits=last[-4:])
        ph.emit()

    def phase_conv(self, j):
        ph = self.ph("conv")
        identb, ev_id = self.load_const(ph, self.c_ident[:, :], [128, 128], BF16)
        cw, ev_cw = self.load_const(ph, self.convw[j], [128, 240], F32, eng="sync")
        cb, ev_cb = self.load_const(ph, self.convb[j], [128, 48], F32, eng="sync")
        diag = ph.sb([128, 240, 128], BF16)
        zt = ph.sb([128, 48, 2], BF16)
        ev_z = ph.op("pool", lambda e: e.memset(zt[:], 0.0), sig=True)
        hal = []
        for u, S in enumerate(self.units):
            hal.append(ph.op("sync", lambda e, u=u: e.dma_start(out=self.xbcT[u][:, :, 0:2].rearrange("c p t -> p c t"), in_=zt[:]),
                             waits=[ev_z], sig="h0", dma=True))
            hal.append(ph.op("sync", lambda e, u=u, S=S: e.dma_start(out=self.xbcT[u][:, :, S + 2:S + 4].rearrange("c p t -> p c t"), in_=zt[:]),
                             waits=[ev_z], sig="h0", dma=True))
        ev_d = None
        for m in range(240):
            ev_d = ph.op("dve", lambda e, m=m: e.tensor_scalar(diag[:, m, :], identb[:], cw[:, m:m + 1], None, ALU.mult),
                         waits=[ev_id, ev_cw], sig=True)
        ub = Ring([ph.sb([128, 48, 516], BF16) for _ in range(2)])
        cv = Ring([ph.sb([128, 512], BF16) for _ in range(3)])
        tm = Ring([ph.sb([128, 4, 40 * 128], BF16) for _ in range(1)])
        psc = Ring([ph.ps([128, 512], F32) for _ in range(3)])
        pst = Ring([ph.ps([128, 512], BF16) for _ in range(2)])
        last = []
        for u, S in enumerate(self.units):
            for kt in range(S // 512):
                t512 = self.uoff[u] + kt * 512
                ku, ubuf, uf = ub.get()
                ev_u = ph.op("sync", lambda e, ubuf=ubuf, u=u, kt=kt: e.dma_start(
                    out=ubuf[:], in_=self.xbcT[u][:, :, kt * 512:kt * 512 + 516].rearrange("c p t -> p c t")),
                    waits=[uf] + hal, sig=f"u{ku}", dma=True)
                ktm, tmb, tmf = tm.get()
                tm_evs = []
                ev_mm = None
                for ct in range(48):
                    kp, pt, pf = psc.get()
                    for k in range(5):
                        ev_mm = ph.op("pe", lambda e, pt=pt, ubuf=ubuf, ct=ct, k=k: e.matmul(
                            pt[:], diag[:, ct * 5 + k, :], ubuf[:, ct, k:k + 512], start=(k == 0), stop=(k == 4)),
                            waits=[ev_u, ev_d, pf], sig=(True if k == 4 else None))
                    kc_, cvb, cvf = cv.get()
                    ev_a = ph.op("act", lambda e, cvb=cvb, pt=pt, ct=ct: e.activation(cvb[:], pt[:], AF.Silu, bias=cb[:, ct:ct + 1]),
                                 waits=[ev_mm, ev_cb, cvf], sig=True)
                    psc.release(kp, ev_a)
                    rel = []
                    if ct >= 32:
                        ev_s = ph.op("sync", lambda e, cvb=cvb, ct=ct, t512=t512: e.dma_start(
                            out=self.bct[ct - 32, :, t512:t512 + 512], in_=cvb[:]), waits=[ev_a], sig=f"sb{kc_}", dma=True)
                        rel.append(ev_s)
                        last.append(ev_s)
                    if ct < 40:
                        kq, ptt, ptf = pst.get()
                        ev_t = None
                        for sub in range(4):
                            ev_t = ph.op("pe", lambda e, ptt=ptt, cvb=cvb, sub=sub: e.transpose(
                                ptt[:, sub * 128:(sub + 1) * 128], cvb[:, sub * 128:(sub + 1) * 128], identb[:]),
                                waits=[ev_a, ptf, ev_id], sig=True)
                        ev_e = ph.op("dve", lambda e, ptt=ptt, tmb=tmb, ct=ct: e.tensor_copy(
                            tmb[:, :, ct * 128:(ct + 1) * 128], ptt[:].rearrange("p (s c) -> p s c", c=128)),
                            waits=[ev_t, tmf], sig=True)
                        pst.release(kq, ev_e)
                        rel.append(ev_t)
                        tm_evs.append(ev_e)
                    cv.release(kc_, rel)
                ub.release(ku, ev_mm)
                e1 = ph.op("sync", lambda e, tmb=tmb, t512=t512: e.dma_start(
                    out=self.xs[t512:t512 + 512, :].rearrange("(s p) c -> p s c", p=128), in_=tmb[:, :, 0:DIN]),
                    waits=tm_evs, sig="sxs", dma=True)
                e2 = ph.op("sync", lambda e, tmb=tmb, t512=t512: e.dma_start(
                    out=self.btm[t512:t512 + 512, :].rearrange("(s p) c -> p s c", p=128), in_=tmb[:, :, DIN:DIN + 1024]),
                    waits=tm_evs, sig="sbt", dma=True)
                tm.release(ktm, [e1, e2])
                last += [e1, e2]
        ph.op("sync", None, waits=last[-24:])
        ph.emit()

    def phase_scan(self, j):
        ph = self.ph("scan")
        identf, ev_idf = self.load_const(ph, self.c_ident[:, :], [128, 128], F32, eng="sync")
        identb, ev_idb = self.load_const(ph, self.c_ident[:, :], [128, 128], BF16)
        tri, ev_tri = self.load_const(ph, self.c_tri[:, :], [128, 256], F32, eng="sync")
        ones, ev_ones = self.load_const(ph, self.c_ones[:, :], [128, 128], F32, eng="sync")
        negm, ev_negm = self.load_const(ph, self.c_negm[:, :], [128, 1024], BF16)
        sel, ev_sel = self.load_const(ph, self.c_sel[:, :], [64, 64 * 128], F32, eng="sync")
        dtbb, ev_dtb = self.load_const(ph, self.dtb[j], [128, 128], F32, eng="sync")
        alb, ev_al = self.load_const(ph, self.alog[j], [128, 128], F32, eng="sync")
        abc = ph.sb([128, 128], F32)
        e0 = ph.op("act", lambda e: e.activation(abc[:], alb[:], AF.Exp), waits=[ev_al], sig=True)
        ev_a = ph.op("dve", lambda e: e.tensor_scalar(abc[:], abc[:], -1.0, None, ALU.mult), waits=[e0], sig=True)
        cdeps = [ev_idf, ev_idb, ev_tri, ev_ones, ev_negm, ev_sel, ev_dtb, ev_a]
        state = ph.sb([128, DIN], F32)
        state_bf = ph.sb([128, DIN], BF16)
        xsb = Ring([ph.sb([128, DIN], BF16) for _ in range(2)])
        btb = Ring([ph.sb([128, 1024], BF16) for _ in range(2)])
        BTb = Ring([ph.sb([128, 8, 128], BF16) for _ in range(2)])
        CTb = Ring([ph.sb([128, 8, 128], BF16) for _ in range(2)])
        dtrb = Ring([ph.sb([128, 128], F32) for _ in range(2)])
        ybuf = Ring([ph.sb([128, DIN], F32) for _ in range(2)])
        xdt = ph.sb([128, DIN], BF16)
        xw = ph.sb([128, DIN], BF16)
        cbm = ph.sb([128, 8, 128], F32)
        sm = {n: ph.sb([128, 64], F32) for n in ("x", "ax", "ex", "ln", "dt", "la", "cs", "ncs", "E", "w", "tot", "cd")}
        csT = ph.sb([64, 128], F32)
        Efull = ph.sb([128, DIN], F32)
        cdfull = ph.sb([128, DIN], F32)
        ncsT = ph.sb([64, 128], F32)
        Lb = Ring([ph.sb([128, 128], F32) for _ in range(3)])
        Mb = Ring([ph.sb([128, 128], BF16) for _ in range(3)])
        tmpo = Ring([ph.sb([128, 512], F32) for _ in range(2)])
        ps_small = Ring([ph.ps([128, 512], F32) for _ in range(1)])
        ps_arg = Ring([ph.ps([128, 512], F32) for _ in range(2)])
        ps_cb = Ring([ph.ps([128, 512], F32) for _ in range(1)])
        ps_y = Ring([ph.ps([128, 512], F32) for _ in range(2)])
        ps_o = Ring([ph.ps([128, 512], F32) for _ in range(1)])
        ps_st = Ring([ph.ps([128, 512], F32) for _ in range(1)])
        last = []
        prev_chunk_done = [None]
        for u, S in enumerate(self.units):
            nchunk = S // 128
            import os as _os
            for d in range(int(_os.environ.get("SCAN_DIRS", "2"))):
                ev_s0 = ph.op("pool", lambda e: e.memset(state[:], 0.0), waits=[prev_chunk_done[0]], sig=True)
                ev_s1 = ph.op("pool", lambda e: e.memset(state_bf[:], 0.0), waits=[prev_chunk_done[0]], sig=True)
                st_ev = [ev_s0, ev_s1]
                order = range(nchunk) if d == 0 else range(nchunk - 1, -1, -1)
                import os as _os
                _mc = int(_os.environ.get("SCAN_MAXCHUNK", "0"))
                _sp = int(_os.environ.get("SCAN_PART", "0"))
                _ss = int(_os.environ.get("SCAN_SUB", "0"))
                if _mc:
                    order = list(order)[:_mc]
                for c in order:
                    t0 = self.uoff[u] + c * 128
                    k1, xs_, f1 = xsb.get()
                    k2, bt_, f2 = btb.get()
                    k3, BT_, f3 = BTb.get()
                    k4, CT_, f4 = CTb.get()
                    k5, dr_, f5 = dtrb.get()
                    l1 = ph.op("sync", lambda e, xs_=xs_, t0=t0: e.dma_start(out=xs_[:], in_=self.xs[t0:t0 + 128, :]), waits=[f1], sig=f"a{k1}", dma=True)
                    l2 = ph.op("sync", lambda e, bt_=bt_, t0=t0: e.dma_start(out=bt_[:], in_=self.btm[t0:t0 + 128, :]), waits=[f2], sig=f"b{k2}", dma=True)
                    l3 = ph.op("sync", lambda e, BT_=BT_, t0=t0: e.dma_start(out=BT_[:], in_=self.bct[0:8, :, t0:t0 + 128].rearrange("g n t -> n g t")), waits=[f3], sig=f"c{k3}", dma=True)
                    l4 = ph.op("sync", lambda e, CT_=CT_, t0=t0: e.dma_start(out=CT_[:], in_=self.bct[8:16, :, t0:t0 + 128].rearrange("g n t -> n g t")), waits=[f4], sig=f"d{k4}", dma=True)
                    l5 = ph.op("sync", lambda e, dr_=dr_, t0=t0: e.dma_start(out=dr_[:], in_=self.dtr[t0:t0 + 128, :]), waits=[f5], sig=f"e{k5}", dma=True)
                    dsl = slice(d * 64, (d + 1) * 64)
                    pc = prev_chunk_done[0]
                    a1 = ph.op("dve", lambda e, dr_=dr_, dsl=dsl: e.tensor_tensor(sm["x"][:], dr_[:, dsl], dtbb[:, dsl], ALU.add), waits=[l5, pc] + cdeps, sig=True)
                    a2 = ph.op("act", lambda e: e.activation(sm["ax"][:], sm["x"][:], AF.Abs), waits=[a1], sig=True)
                    a3 = ph.op("act", lambda e: e.activation(sm["ex"][:], sm["ax"][:], AF.Exp, scale=-1.0), waits=[a2], sig=True)
                    a4 = ph.op("act", lambda e: e.activation(sm["ln"][:], sm["ex"][:], AF.Ln, bias=ones[:, 0:1]), waits=[a3], sig=True)
                    a5 = ph.op("dve", lambda e: e.scalar_tensor_tensor(sm["dt"][:], sm["x"][:], 0.0, sm["ln"][:], ALU.max, ALU.add), waits=[a4], sig=True)
                    dtrb.release(k5, a1)
                    a6 = ph.op("dve", lambda e, dsl=dsl: e.tensor_tensor(sm["la"][:], sm["dt"][:], abc[:, dsl], ALU.mult), waits=[a5], sig=True)
                    if _sp == 1:
                        continue
                    kps, pss_, psf = ps_small.get()
                    m1 = ph.op("pe", lambda e, pss_=pss_, d=d: e.matmul(pss_[:, 0:64], tri[:, d * 128:(d + 1) * 128], sm["la"][:], start=True, stop=True),
                               waits=[a6, psf], sig=True)
                    m2 = ph.op("pe", lambda e, pss_=pss_: e.matmul(pss_[:, 64:128], ones[:], sm["la"][:], start=True, stop=True), waits=[a6], sig=True)
                    b1 = ph.op("dve", lambda e, pss_=pss_: e.tensor_copy(sm["cs"][:], pss_[:, 0:64]), waits=[m1, m2], sig=True)
                    b2 = ph.op("dve", lambda e, pss_=pss_: e.tensor_copy(sm["tot"][:], pss_[:, 64:128]), waits=[m2], sig=True)
                    b3 = ph.op("dve", lambda e: e.tensor_scalar(sm["ncs"][:], sm["cs"][:], -1.0, None, ALU.mult), waits=[b1], sig=True)
                    b4 = ph.op("act", lambda e: e.activation(sm["E"][:], sm["cs"][:], AF.Exp), waits=[b1], sig=True)
                    b5 = ph.op("dve", lambda e: e.tensor_tensor(sm["w"][:], sm["tot"][:], sm["cs"][:], ALU.subtract), waits=[b1, b2], sig=True)
                    b6 = ph.op("act", lambda e: e.activation(sm["w"][:], sm["w"][:], AF.Exp), waits=[b5], sig=True)
                    b7 = ph.op("act", lambda e: e.activation(sm["cd"][:], sm["tot"][:], AF.Exp), waits=[b2], sig=True)
                    b4f = ph.op("dve", lambda e: e.tensor_copy(Efull[:].rearrange("p (h q) -> p h q", q=64), bcl(sm["E"][:], 64)), waits=[b4, pc], sig=True)
                    b7f = ph.op("dve", lambda e: e.tensor_copy(cdfull[:].rearrange("p (h q) -> p h q", q=64), bcl(sm["cd"][:], 64)), waits=[b7, pc], sig=True)
                    m3 = ph.op("pe", lambda e, pss_=pss_: e.transpose(pss_[0:64, 128:256], sm["cs"][:], identf[:]), waits=[b1, b2], sig=True)
                    b8 = ph.op("dve", lambda e, pss_=pss_: e.tensor_copy(csT[:], pss_[0:64, 128:256]), waits=[m3], sig=True)
                    b9 = ph.op("dve", lambda e: e.tensor_scalar(ncsT[:], csT[:], -1.0, None, ALU.mult), waits=[b8], sig=True)
                    ps_small.release(kps, b8)
                    if _sp == 2:
                        continue
                    x1 = ph.op("dve", lambda e, xs_=xs_: e.tensor_tensor(
                        xdt[:].rearrange("p (h q) -> p h q", q=64), xs_[:].rearrange("p (h q) -> p h q", q=64),
                        bcl(sm["dt"][:], 64), ALU.mult), waits=[l1, a5, pc], sig=True)
                    x2 = ph.op("dve", lambda e: e.tensor_tensor(
                        xw[:].rearrange("p (h q) -> p h q", q=64), xdt[:].rearrange("p (h q) -> p h q", q=64),
                        bcl(sm["w"][:], 64), ALU.mult), waits=[x1, b6], sig=True)
                    xsb.release(k1, x1)
                    if _sp == 3:
                        continue
                    cb_evs = []
                    for g in range(8):
                        kcb, pcb, pcf = ps_cb.get()
                        m4 = ph.op("pe", lambda e, pcb=pcb, BT_=BT_, CT_=CT_, g=g: e.matmul(pcb[:, 0:128], BT_[:, g, :], CT_[:, g, :], start=True, stop=True),
                                   waits=[l3, l4, pcf], sig=True)
                        c1 = ph.op("dve", lambda e, pcb=pcb, g=g, d=d: e.tensor_tensor(cbm[:, g, :], pcb[:, 0:128], tri[:, d * 128:(d + 1) * 128], ALU.mult),
                                   waits=[m4, pc], sig=True)
                        ps_cb.release(kcb, c1)
                        cb_evs.append(c1)
                    BTb.release(k3, m4)
                    if _sp == 4:
                        continue
                    ky, yb_, yf_ = ybuf.get()
                    y_evs = []
                    ev_ymm_last = None
                    for g in range(8):
                        kpy, py, pyf = ps_y.get()
                        for hq in range(2):
                            kpa, pa, paf = ps_arg.get()
                            for hh in range(4):
                                h = g * 8 + hq * 4 + hh
                                n1 = ph.op("pe", lambda e, pa=pa, d=d, hh=hh: e.matmul(pa[:, hh * 128:(hh + 1) * 128], identb[:], negm[:, d * 512:d * 512 + 128], start=True, stop=False),
                                           waits=[paf] + cdeps, sig=None)
                                ph.op("pe", lambda e, pa=pa, h=h, hh=hh: e.matmul(
                                    pa[:, hh * 128:(hh + 1) * 128], sel[:, h * 128:(h + 1) * 128], csT[:], start=False, stop=False),
                                    waits=[b8])
                                n2 = ph.op("pe", lambda e, pa=pa, h=h, hh=hh: e.matmul(
                                    pa[:, hh * 128:(hh + 1) * 128], ncsT[:], sel[:, h * 128:(h + 1) * 128], start=False, stop=True),
                                    waits=[b9], sig=(True if hh == 3 else None))
                            for hh in range(4):
                                if _ss == 9:
                                    continue
                                h = g * 8 + hq * 4 + hh
                                kl, lb, lf = Lb.get()
                                _n3 = _os.environ.get("SCAN_N3", "nobias")
                                if _n3 == "bias":
                                    n3 = ph.op("act", lambda e, lb=lb, pa=pa, h=h, hh=hh: e.activation(
                                        lb[:], pa[:, hh * 128:(hh + 1) * 128], AF.Exp, bias=sm["ncs"][:, h:h + 1]),
                                        waits=[n2, b3, lf], sig=True)
                                elif _n3 == "nobias":
                                    n3 = ph.op("act", lambda e, lb=lb, pa=pa, h=h, hh=hh: e.activation(
                                        lb[:], pa[:, hh * 128:(hh + 1) * 128], AF.Exp),
                                        waits=[n2, b3, lf], sig=True)
                                else:
                                    n3a = ph.op("dve", lambda e, lb=lb, pa=pa, h=h, hh=hh: e.tensor_scalar(
                                        lb[:], pa[:, hh * 128:(hh + 1) * 128], sm["cs"][:, h:h + 1], None, ALU.subtract),
                                        waits=[n2, b1, lf], sig=True)
                                    n3 = ph.op("act", lambda e, lb=lb: e.activation(lb[:], lb[:], AF.Exp), waits=[n3a], sig=True)
                                if _ss == 1:
                                    continue
                                km, mb, mf = Mb.get()
                                n4 = ph.op("dve", lambda e, mb=mb, lb=lb, g=g: e.tensor_tensor(mb[:], lb[:], cbm[:, g, :], ALU.mult),
                                           waits=[n3, cb_evs[g], mf], sig=True)
                                Lb.release(kl, n4)
                                if _ss == 2:
                                    continue
                                hi = hq * 4 + hh
                                n5 = ph.op("pe", lambda e, py=py, mb=mb, h=h, hi=hi: e.matmul(
                                    py[:, hi * 64:(hi + 1) * 64], mb[:], xdt[:, h * 64:(h + 1) * 64], start=True, stop=True),
                                    waits=[n4, x1, pyf], sig=True)
                                Mb.release(km, n5)
                                ev_ymm_last = n5
                            ps_arg.release(kpa, n3 if _ss != 9 else n2)
                        if _ss in (1, 2, 3, 9):
                            continue
                        kpo, po, pof = ps_o.get()
                        o1 = ph.op("pe", lambda e, po=po, CT_=CT_, g=g: e.matmul(po[:], CT_[:, g, :], state_bf[:, g * 512:(g + 1) * 512], start=True, stop=True),
                                   waits=[l4, pof] + st_ev, sig=True)
                        if _ss == 4:
                            continue
                        kt_, tb, tf = tmpo.get()
                        o2a = ph.op("act", lambda e, tb=tb, po=po: e.copy(tb[:], po[:]), waits=[o1, tf], sig=True)
                        ps_o.release(kpo, o2a)
                        o2b = ph.op("act", lambda e, yb_=yb_, py=py, g=g: e.copy(yb_[:, g * 512:(g + 1) * 512], py[:]),
                                    waits=[ev_ymm_last, yf_], sig=True)
                        ps_y.release(kpy, o2b)
                        o2 = ph.op("dve", lambda e, tb=tb, g=g: e.tensor_tensor(
                            tb[:], tb[:], Efull[:, g * 512:(g + 1) * 512], ALU.mult),
                            waits=[o2a, b4f], sig=True)
                        o3 = ph.op("dve", lambda e, yb_=yb_, tb=tb, g=g: e.tensor_tensor(
                            yb_[:, g * 512:(g + 1) * 512], yb_[:, g * 512:(g + 1) * 512], tb[:], ALU.add),
                            waits=[o2, o2b], sig=True)
                        tmpo.release(kt_, o3)
                        y_evs.append(o3)
                    if _sp == 5:
                        continue
                    CTb.release(k4, o1)
                    new_st = []
                    for g in range(8):
                        kst, pst_, pstf = ps_st.get()
                        s1 = ph.op("pe", lambda e, pst_=pst_, bt_=bt_, g=g: e.matmul(pst_[:], bt_[:, g * 128:(g + 1) * 128], xw[:, g * 512:(g + 1) * 512], start=True, stop=True),
                                   waits=[l2, x2, pstf], sig=True)
                        s2 = ph.op("dve", lambda e, g=g: e.tensor_tensor(
                            state[:, g * 512:(g + 1) * 512], state[:, g * 512:(g + 1) * 512],
                            cdfull[:, g * 512:(g + 1) * 512], ALU.mult),
                            waits=[b7f, y_evs[-1]] + st_ev, sig=True)
                        s3 = ph.op("dve", lambda e, pst_=pst_, g=g: e.tensor_tensor(
                            state[:, g * 512:(g + 1) * 512], state[:, g * 512:(g + 1) * 512], pst_[:], ALU.add), waits=[s1, s2], sig=True)
                        ps_st.release(kst, s3)
                        s4 = ph.op("act", lambda e, g=g: e.copy(state_bf[:, g * 512:(g + 1) * 512], state[:, g * 512:(g + 1) * 512]),
                                   waits=[s3, y_evs[-1]], sig=True)
                        new_st += [s3, s4]
                    btb.release(k2, s1)
                    st_ev = [new_st[-2], new_st[-1]]
                    ev_st = ph.op("sync", lambda e, yb_=yb_, d=d, t0=t0: e.dma_start(out=self.yfb[d][t0:t0 + 128, :], in_=yb_[:]),
                                  waits=y_evs, sig=f"sy{ky}", dma=True)
                    ybuf.release(ky, ev_st)
                    last.append(ev_st)
                    prev_chunk_done[0] = new_st[-1]
                    prev_chunk_done[0] = ph.op("dve", lambda e: e.tensor_copy(sm["x"][:, 0:1], sm["x"][:, 0:1]),
                                               waits=[new_st[-1], new_st[-2], ev_ymm_last, s1], sig=True)
        ph.op("sync", None, waits=last[-2:])
        ph.emit()

    def phase_final(self):
        ph = self.ph("fin")
        src = self.xres[:, :]
        dbg = getattr(self, "debug_out", None)
        if dbg is not None:
            src = dbg(self)
        dst = self.y_out[:, :]
        if isinstance(src, tuple):
            src, dst = src
        ev = ph.op("pool", lambda e: e.dma_start(out=dst, in_=src), sig="f", dma=True)
        ph.op("sync", None, waits=[ev])
        ph.emit()


UNITS = (8192, 2048)
DEPTH = 4
_CACHE = {}


def _get_nc(units, depth, stage_limit, depth_full, debug_out=None):
    key = (units, depth, stage_limit, depth_full, debug_out)
    if key not in _CACHE:
        b = Builder(list(units), depth, stage_limit)
        b.depth_full = depth_full
        b.debug_out = debug_out
        b.build()
        _CACHE[key] = b
    return _CACHE[key]


def run_units(xa_list, xb_list, w, units=UNITS, depth=DEPTH, stage_limit=None, depth_full=DEPTH, trace=False, ncores=8, debug_out=None):
    b = _get_nc(tuple(units), depth, stage_limit, depth_full, debug_out)
    consts = host_consts()
    ns = max(depth // 2, 1)
    f = lambda a: np.ascontiguousarray(a, dtype=np.float32)
    common = {
        "attn_w_qkv": f(w["attn_w_qkv"][: (depth + 1) // 2]),
        "attn_w_o": f(w["attn_w_o"][: (depth + 1) // 2]),
        "ssd_w_in": f(w["ssd_w_in"][:ns]),
        "ssd_w_out": f(w["ssd_w_out"][:ns]),
        "mlp_w1": f(w["mlp_w1"][:depth]),
        "mlp_w2": f(w["mlp_w2"][:depth]),
        "ln_g": f(w["ln_g"][:depth].reshape(-1, D)),
        "ln_b": f(w["ln_b"][:depth].reshape(-1, D)),
        "convw": f(np.stack([w["ssd_conv_w"][k].T.reshape(48, 128, 5).transpose(1, 0, 2).reshape(128, 240) for k in range(ns)])),
        "convb": f(np.stack([w["ssd_conv_b"][k].reshape(48, 128).T for k in range(ns)])),
        "dtb": f(np.stack([np.broadcast_to(w["ssd_dt_bias"][k].reshape(1, 128), (128, 128)) for k in range(ns)])),
        "alog": f(np.stack([np.broadcast_to(w["ssd_a_log"][k].reshape(1, 128), (128, 128)) for k in range(ns)])),
        "dsk": f(np.stack([np.broadcast_to(w["ssd_d"][k].reshape(1, 64), (128, 64)) for k in range(ns)])),
        "normw": f(np.stack([w["ssd_norm_w"][k].reshape(32, 128).T for k in range(ns)])),
        "c_ident": consts["ident"], "c_tri": consts["tri"], "c_ones": consts["ones"], "c_negm": consts["negm"],
        "c_sel": consts["sel"], "c_absrel": consts["absrel"],
    }
    in_maps = []
    for c in range(ncores):
        m = dict(common)
        m["x_in"] = np.ascontiguousarray(np.concatenate([xa_list[c], xb_list[c]], 0), dtype=np.float32)
        in_maps.append(m)
    res = run_bass_kernel_spmd(b.nc, in_maps, core_ids=list(range(ncores)), trace=trace)
    return [r["y_out"] for r in res.results], res


def kernel(x_prompt, x_sample, attn_w_qkv, attn_w_o, ssd_w_in, ssd_conv_w, ssd_conv_b, ssd_dt_bias,
           ssd_a_log, ssd_d, ssd_norm_w, ssd_w_out, mlp_w1, mlp_w2, ln_g, ln_b):
    w = dict(attn_w_qkv=attn_w_qkv, attn_w_o=attn_w_o, ssd_w_in=ssd_w_in, ssd_conv_w=ssd_conv_w, ssd_conv_b=ssd_conv_b,
             ssd_dt_bias=ssd_dt_bias, ssd_a_log=ssd_a_log, ssd_d=ssd_d, ssd_norm_w=ssd_norm_w, ssd_w_out=ssd_w_out,
             mlp_w1=mlp_w1, mlp_w2=mlp_w2, ln_g=ln_g, ln_b=ln_b)
    x_prompt = np.asarray(x_prompt, np.float32)
    x_sample = np.asarray(x_sample, np.float32)
    SA, SB = UNITS
    za = np.zeros((SA, D), np.float32)
    zb = np.zeros((SB, D), np.float32)
    xa = [x_sample[c] if c < 2 else za for c in range(8)]
    xb = [x_prompt[c - 2] if 2 <= c < 6 else zb for c in range(8)]
    outs, _ = run_units(xa[:6], xb[:6], w, ncores=6)
    y_sample = np.stack([outs[c][:SA] for c in range(2)], 0)
    y_prompt = np.stack([outs[c][SA:SA + SB] for c in range(2, 6)], 0)
    return (y_prompt, y_sample)
```

```python
import math
import sys
from contextlib import ExitStack
import numpy as np
import concourse.bass as bass
import concourse.mybir as mybir
from concourse.bass_utils import run_bass_kernel_spmd

F32 = mybir.dt.float32
BF16 = mybir.dt.bfloat16
ALU = mybir.AluOpType
AF = mybir.ActivationFunctionType

D = 2048
KC = 16
NGRP = 3
DILS = (1, 4, 16)
NH = 16
HD = 128
QKVD = 9 * D
DIN = 4096
CONVD = 6144
INPD = 10368
SH = 64
SP = 64
SG = 8
SN = 128
DFF = 8192
LN_EPS = 1e-5
def bcl(ap, n):
    sh = list(ap.shape)
    return ap.rearrange("p (h o) -> p h o", o=1).broadcast_to([sh[0], sh[1], n])


ENGS = ("sync", "act", "pool", "dve", "pe")
NEGM = -30000.0
ABIG = 1.0e6


class Ph:
    def __init__(self, nc, name, gb=None):
        self.nc = nc
        self.name = name
        self.ops = {e: [] for e in ENGS}
        self.gb = gb
        self.semcnt = gb.gsemcnt
        self.base = dict(gb.gsemcnt)
        self.dmap = {}
        self.barrier = list(gb.barrier)
        self.st = ExitStack()
        self.nbuf = 0
        self.psums = []

    def sb(self, shape, dt):
        self.nbuf += 1
        return self.st.enter_context(self.nc.sbuf_tensor(f"{self.name}_b{self.nbuf}", list(shape), dt))

    def ps(self, shape, dt=F32):
        self.nbuf += 1
        t = self.st.enter_context(self.nc.psum_tensor(f"{self.name}_p{self.nbuf}", list(shape), dt))
        self.psums.append((t, dt))
        return t

    def op(self, eng, fn, waits=(), sig=None, dma=False):
        ev = None
        inc = 16 if dma else 1
        if sig is True:
            sig = eng
        if sig is not None and sig not in ENGS:
            if sig not in self.dmap:
                self.dmap[sig] = f"d{len(self.dmap)}"
            sig = self.dmap[sig]
        if sig is not None:
            c = self.semcnt.get(sig, 0) + inc
            self.semcnt[sig] = c
            ev = (sig, c)
        ws = list(self.barrier)
        for w in waits:
            if w is None:
                continue
            if isinstance(w, list):
                ws.extend([x for x in w if x is not None])
            else:
                ws.append(w)
        self.ops[eng].append((fn, ws, ev, inc, sys._getframe(1).f_lineno))
        return ev

    def check_deadlock(self):
        cnt = dict(self.base)
        pos = {e: 0 for e in ENGS}
        progress = True
        while progress:
            progress = False
            for e in ENGS:
                while pos[e] < len(self.ops[e]):
                    fn, ws, ev, inc, ln = self.ops[e][pos[e]]
                    if all(cnt.get(s_, 0) >= c_ for (s_, c_) in ws):
                        if ev is not None:
                            cnt[ev[0]] = cnt.get(ev[0], 0) + inc
                        pos[e] += 1
                        progress = True
                    else:
                        break
        stuck = {e: pos[e] for e in ENGS if pos[e] < len(self.ops[e])}
        if stuck:
            msg = []
            for e, p in stuck.items():
                fn, ws, ev, inc, ln = self.ops[e][p]
                bad = [(s_, c_, cnt.get(s_, 0)) for (s_, c_) in ws if cnt.get(s_, 0) < c_]
                msg.append(f"{e}@{p}/{len(self.ops[e])} line {ln} waits {bad}")
            raise RuntimeError(f"DEADLOCK in phase {self.name}: " + "; ".join(msg))

    def emit(self):
        nc = self.nc
        gb = self.gb
        bt = gb.bar_f
        e_d = self.op("dve", lambda e: e.memset(bt[:, 0:1], 0.0), sig=True)
        e_a = self.op("act", lambda e: e.copy(bt[:, 1:2], bt[:, 2:3]), sig=True)
        e_p = self.op("pool", lambda e: e.memset(bt[:, 3:4], 0.0), sig=True)
        evs = [e_d, e_a, e_p]
        if self.psums:
            pt, pdt = self.psums[0]
            if pdt == F32:
                e_t = self.op("pe", lambda e: e.matmul(pt[:, 0:8], gb.bar_b[:, 0:128], gb.bar_b[:, 128:136], start=True, stop=True),
                              waits=evs, sig=True)
            else:
                e_t = self.op("pe", lambda e: e.transpose(pt[:, 0:128], gb.bar_b[:, 0:128], gb.bar_b[:, 0:128]), waits=evs, sig=True)
            evs.append(e_t)
        e_s = self.op("sync", lambda e: e.dma_start(out=gb.bar_d[1:2, :], in_=gb.bar_d[0:1, :]), waits=evs, sig="bsync", dma=True)
        evs.append(e_s)
        gb.barrier = evs
        self.check_deadlock()
        if True:
            for n in self.semcnt:
                if n not in gb.gsems:
                    gb.gsems[n] = gb.gst.enter_context(nc.semaphore(f"g_{n}"))
            sems = gb.gsems
            block = gb.block

            def mk(e):
                def body(eng):
                    waited = gb.waited[e]
                    for fn, waits, ev, inc, _ln in self.ops[e]:
                        for (s, c) in waits:
                            if waited.get(s, 0) < c:
                                eng.wait_ge(sems[s], c)
                                waited[s] = c
                        if fn is None:
                            continue
                        ins = fn(eng)
                        if ev is not None:
                            ins.then_inc(sems[ev[0]], inc)
                return body

            block.sync(mk("sync"))
            block.scalar(mk("act"))
            block.gpsimd(mk("pool"))
            block.vector(mk("dve"))
            block.tensor(mk("pe"))
        self.st.close()


class Ring:
    def __init__(self, bufs):
        self.bufs = bufs
        self.free = [None] * len(bufs)
        self.i = 0

    def get(self):
        k = self.i % len(self.bufs)
        self.i += 1
        return k, self.bufs[k], self.free[k]

    def release(self, k, ev):
        self.free[k] = ev


def host_consts():
    c = {}
    c["ident"] = np.eye(128, dtype=np.float32)
    k = np.arange(128)[:, None]
    t = np.arange(128)[None, :]
    tri_f = (k <= t).astype(np.float32)
    tri_b = (k >= t).astype(np.float32)
    c["tri"] = np.stack([tri_f, tri_b], 1).reshape(128, 256)
    c["ones"] = np.ones((128, 128), np.float32)
    nm = np.stack([np.where(k <= t, 0.0, NEGM), np.where(k >= t, 0.0, NEGM)], 0).astype(np.float32)
    c["negm"] = np.concatenate([np.tile(nm[0], (1, 4)), np.tile(nm[1], (1, 4))], 1)
    sel = np.zeros((64, 64, 128), np.float32)
    for h in range(64):
        sel[h, h, :] = 1.0
    c["sel"] = sel.reshape(64, 64 * 128)
    kk = np.arange(128)[:, None].astype(np.float64)
    qq = np.arange(128)[None, :].astype(np.float64)
    relA = 64 + kk - qq
    relB = kk - 64 - qq
    vA = kk <= qq
    vB = kk >= qq

    def mk(vA_, vB_):
        a = np.where(vA_, np.abs(relA), ABIG)
        b = np.where(vB_, np.abs(relB), ABIG)
        return np.concatenate([a, b], 1).astype(np.float32)

    inter = mk(vA, vB)
    first = mk(vA & False, vB & (kk >= 64))
    last = mk(vA & (kk < 64), vB & False)
    both = mk(vA & False, vB & False)
    c["absrel"] = np.concatenate([inter, first, last, both], 1)
    return c


class Builder:
    def __init__(self, units, depth, stage_limit=None):
        self.units = units
        self.T = sum(units)
        self.depth = depth
        self.uoff = [sum(units[:i]) for i in range(len(units))]
        self.NT = self.T // 512
        self.stage_limit = stage_limit
        self.nc = bass.Bass("TRN2", target_bir_lowering=False)
        self.pi = 0
        self.gsemcnt = {}
        self.gsems = {}
        self.barrier = []
        self.waited = {e: {} for e in ENGS}
        self.gst = ExitStack()

    def din(self, name, shape, dt=F32):
        return self.nc.dram_tensor(name, list(shape), dt, kind="ExternalInput").ap()

    def dscr(self, name, shape, dt):
        if name in getattr(self, "ext_scratch", ()):
            return self.nc.dram_tensor(name, list(shape), dt, kind="ExternalInput").ap()
        return self.nc.dram_tensor(name, list(shape), dt).ap()

    def ph(self, name):
        self.pi += 1
        return Ph(self.nc, f"p{self.pi}{name}", self)

    def build(self):
        nc = self.nc
        T = self.T
        na, ns = (self.depth + 1) // 2, self.depth // 2
        self.x_in = self.din("x_in", [T, D])
        self.w_qkv = self.din("attn_w_qkv", [na, D, QKVD])
        self.w_o = self.din("attn_w_o", [na, D, D])
        self.w_in = self.din("ssd_w_in", [max(ns, 1), D, INPD])
        self.w_out = self.din("ssd_w_out", [max(ns, 1), DIN, D])
        self.w1 = self.din("mlp_w1", [self.depth, D, DFF])
        self.w2 = self.din("mlp_w2", [self.depth, DFF, D])
        self.lng = self.din("ln_g", [self.depth * 2, D])
        self.lnb = self.din("ln_b", [self.depth * 2, D])
        self.convw = self.din("convw", [max(ns, 1), 128, 48 * 5])
        self.convb = self.din("convb", [max(ns, 1), 128, 48])
        self.dtb = self.din("dtb", [max(ns, 1), 128, 128])
        self.alog = self.din("alog", [max(ns, 1), 128, 128])
        self.dsk = self.din("dsk", [max(ns, 1), 128, 64])
        self.normw = self.din("normw", [max(ns, 1), 128, 32])
        self.c_ident = self.din("c_ident", [128, 128])
        self.c_tri = self.din("c_tri", [128, 256])
        self.c_ones = self.din("c_ones", [128, 128])
        self.c_negm = self.din("c_negm", [128, 1024])
        self.c_sel = self.din("c_sel", [64, 64 * 128])
        self.c_absrel = self.din("c_absrel", [128, 1024])
        self.y_out = nc.dram_tensor("y_out", [T, D], F32, kind="ExternalOutput").ap()
        self.xres = self.dscr("xres", [T, D], F32)
        self.xT = self.dscr("xT", [self.NT, 128, KC, 512], BF16)
        self.qkvg = [self.dscr(f"qkv{g}", [T, 3 * D], BF16) for g in range(NGRP)]
        self.og = [self.dscr(f"og{g}", [T, NH * 129], F32) for g in range(NGRP)]
        self.zz = self.dscr("zz", [T, DIN], F32)
        self.dtr = self.dscr("dtr", [T, 128], F32)
        self.xbcT = [self.dscr(f"xbcT{u}", [48, 128, S + 4], BF16) for u, S in enumerate(self.units)]
        self.xs = self.dscr("xs", [T, DIN], BF16)
        self.btm = self.dscr("btm", [T, 1024], BF16)
        self.bct = self.dscr("bct", [16, 128, T], BF16)
        self.yfb = [self.dscr(f"yfb{d}", [T, DIN], F32) for d in range(2)]

        stages = []
        stages.append(("init", lambda: self.phase_init()))
        for i in range(self.depth):
            j = i // 2
            if i % 2 == 0:
                stages.append((f"qkv{i}", lambda j=j: self.phase_qkv(j)))
                stages.append((f"att{i}", lambda j=j: self.phase_att(j)))
                stages.append((f"oproj{i}", lambda i=i, j=j: self.phase_proj_ln("oproj", i, j)))
            else:
                stages.append((f"inproj{i}", lambda j=j: self.phase_inproj(j)))
                stages.append((f"conv{i}", lambda j=j: self.phase_conv(j)))
                stages.append((f"scan{i}", lambda j=j: self.phase_scan(j)))
                stages.append((f"gate{i}", lambda i=i, j=j: self.phase_proj_ln("gate", i, j)))
            stages.append((f"mlp{i}", lambda i=i: self.phase_proj_ln("mlp", i, i)))
        if self.stage_limit is not None:
            stages = stages[: self.stage_limit]
        only = getattr(self, "only", None)
        import os as _os
        if _os.environ.get("ONLY_STAGES"):
            only = tuple(_os.environ["ONLY_STAGES"].split(","))
        if only is not None:
            stages = [st_ for st_ in stages if st_[0] in only]
        self.bar_d = self.dscr("bar_d", [2, 16], F32)
        self.bar_f = self.gst.enter_context(nc.sbuf_tensor("bar_f", [128, 8], F32))
        self.bar_b = self.gst.enter_context(nc.sbuf_tensor("bar_b", [128, 136], BF16))
        self.block = self.gst.enter_context(nc.Block())
        ph0 = self.ph("zero")
        ph0.op("pool", lambda e: e.memset(self.bar_f[:], 0.0), sig=True)
        ph0.op("pool", lambda e: e.memset(self.bar_b[:], 0.0), sig=True)
        ph0.emit()
        for name, fn in stages:
            fn()
        self.phase_final()
        self.gst.close()
        return nc

    def load_const(self, ph, src, shape, dt, eng="pool"):
        buf = ph.sb(shape, dt)
        ev = ph.op(eng, lambda e: e.dma_start(out=buf[:], in_=src), sig=f"c{ph.nbuf}", dma=True)
        return buf, ev

    def phase_init(self):
        ph = self.ph("init")
        T = self.T
        ident, ev_id = self.load_const(ph, self.c_ident[:, :], [128, 128], BF16)
        xin = Ring([ph.sb([128, D], F32) for _ in range(2)])
        xb = Ring([ph.sb([128, D], BF16) for _ in range(2)])
        stg = Ring([ph.sb([128, KC, 512], BF16) for _ in range(2)])
        pst = Ring([ph.ps([128, D], BF16) for _ in range(2)])
        ev_copy = ph.op("sync", lambda e: e.dma_start(out=self.xres[:, :], in_=self.x_in[:, :]), sig="cp", dma=True)
        last = [ev_copy]
        for nt in range(self.NT):
            ks, sbuf, sfree = stg.get()
            evs = []
            for sub in range(4):
                t0 = nt * 512 + sub * 128
                k1, b1, f1 = xin.get()
                ev_ld = ph.op("sync", lambda e, b1=b1, t0=t0: e.dma_start(out=b1[:], in_=self.x_in[t0:t0 + 128, :]),
                              waits=[f1], sig=f"ld{k1}", dma=True)
                k2, b2, f2 = xb.get()
                ev_cv = ph.op("act", lambda e, b1=b1, b2=b2: e.copy(b2[:], b1[:]), waits=[ev_ld, f2], sig=True)
                xin.release(k1, ev_cv)
                ev_t = self.emit_transposes(ph, b2, KC, ident, ev_id, pst, sbuf, sub, [ev_cv, sfree])
                xb.release(k2, ev_t["pe"])
                evs.append(ev_t["evac"])
            ev_st = ph.op("sync", lambda e, sbuf=sbuf, nt=nt: e.dma_start(out=self.xT[nt], in_=sbuf[:]),
                          waits=evs, sig=f"st{ks}", dma=True)
            stg.release(ks, ev_st)
            last.append(ev_st)
        ph.op("sync", None, waits=last)
        ph.emit()

    def emit_transposes(self, ph, src_bf, nchunk, ident, ev_id, pst, dst, sub, waits, scale=None, scale_ev=None):
        res = {}
        for c0 in range(0, nchunk, 16):
            kp, pt, pf = pst.get()
            ev_pe = None
            for c in range(c0, min(c0 + 16, nchunk)):
                ev_pe = ph.op("pe", lambda e, pt=pt, c=c, c0=c0: e.transpose(
                    pt[:, (c - c0) * 128:(c - c0 + 1) * 128], src_bf[:, c * 128:(c + 1) * 128], ident[:]),
                    waits=[ev_id, pf] + list(waits), sig=True)
            n = min(16, nchunk - c0)
            if scale is None:
                ev_ev = ph.op("dve", lambda e, pt=pt, c0=c0, n=n: e.tensor_copy(
                    dst[:, c0:c0 + n, sub * 128:(sub + 1) * 128],
                    pt[:, 0:n * 128].rearrange("p (c t) -> p c t", t=128)),
                    waits=[ev_pe], sig=True)
            else:
                for c in range(c0, c0 + n):
                    ev_ev = ph.op("dve", lambda e, pt=pt, c0=c0, c=c: e.tensor_scalar(
                        dst[:, c, sub * 128:(sub + 1) * 128], pt[:, (c - c0) * 128:(c - c0 + 1) * 128],
                        scale[:, c:c + 1], None, ALU.mult), waits=[ev_pe, scale_ev], sig=True)
            pst.release(kp, ev_ev)
            res["pe"] = ev_pe
            res["evac"] = ev_ev
        return res

    def phase_qkv(self, j):
        ph = self.ph("qkv")
        W = self.w_qkv[j]
        xt = Ring([ph.sb([128, KC, 512], BF16) for _ in range(2)])
        wsl = Ring([ph.sb([128, KC, 512], BF16) for _ in range(4)])
        stg = Ring([ph.sb([128, 4, 512], BF16) for _ in range(3)])
        pss = Ring([ph.ps([128, 512], F32) for _ in range(4)])
        last = []
        ecnt = 0
        for nt in range(self.NT):
            kx, xb, xf = xt.get()
            ev_x = ph.op("sync", lambda e, xb=xb, nt=nt: e.dma_start(out=xb[:], in_=self.xT[nt]),
                         waits=[xf], sig=f"x{kx}", dma=True)
            ev_lastmm = None
            for nsl in range(QKVD // 512):
                kw, wb, wf = wsl.get()
                ev_w = ph.op("pool", lambda e, wb=wb, nsl=nsl: e.dma_start(
                    out=wb[:], in_=W[:, nsl * 512:(nsl + 1) * 512].rearrange("(c p) n -> p c n", p=128)),
                    waits=[wf], sig=f"w{kw}", dma=True)
                ksg, sg, sgf = stg.get()
                evs = []
                for sub in range(4):
                    kp, pt, pf = pss.get()
                    for c in range(KC):
                        ev_mm = ph.op("pe", lambda e, pt=pt, xb=xb, wb=wb, c=c, sub=sub: e.matmul(
                            pt[:], xb[:, c, sub * 128:(sub + 1) * 128], wb[:, c, :], start=(c == 0), stop=(c == KC - 1)),
                            waits=[ev_x, ev_w, pf], sig=(True if c == KC - 1 else None))
                    ecnt += 1
                    if ecnt % 2 == 0:
                        ev_e = ph.op("act", lambda e, pt=pt, sg=sg, sub=sub: e.copy(sg[:, sub, :], pt[:]),
                                     waits=[ev_mm, sgf], sig=True)
                    else:
                        ev_e = ph.op("dve", lambda e, pt=pt, sg=sg, sub=sub: e.tensor_copy(sg[:, sub, :], pt[:]),
                                     waits=[ev_mm, sgf], sig=True)
                    pss.release(kp, ev_e)
                    evs.append(ev_e)
                    ev_lastmm = ev_mm
                wsl.release(kw, ev_lastmm)
                ev_st = ph.op("sync", lambda e, sg=sg, nt=nt, nsl=nsl: e.dma_start(
                    out=self.qkvg[nsl // 12][nt * 512:(nt + 1) * 512, (nsl % 12) * 512:(nsl % 12 + 1) * 512].rearrange("(s p) n -> p s n", p=128),
                    in_=sg[:]), waits=evs, sig=f"s{ksg}", dma=True)
                stg.release(ksg, ev_st)
                last.append(ev_st)
            xt.release(kx, ev_lastmm)
        ph.op("sync", None, waits=last[-3:])
        ph.emit()

    def phase_att(self, j):
        ph = self.ph("att")
        scale = HD ** -0.5
        ident, ev_id = self.load_const(ph, self.c_ident[:, :], [128, 128], BF16)
        absrel, ev_ar = self.load_const(ph, self.c_absrel[:, :], [128, 1024], F32, eng="sync")
        kbuf = Ring([ph.sb([128, D], BF16) for _ in range(2)])
        qbuf = Ring([ph.sb([128, D], BF16) for _ in range(2)])
        vbuf = Ring([ph.sb([128, NH, 129], BF16) for _ in range(2)])
        kT = Ring([ph.sb([128, NH, 128], BF16) for _ in range(2)])
        qT = Ring([ph.sb([128, NH, 128], BF16) for _ in range(3)])
        acc = Ring([ph.sb([128, NH, 129], F32) for _ in range(3)])
        tbuf = Ring([ph.sb([128, 256], F32) for _ in range(3)])
        pbuf = Ring([ph.sb([128, 256], BF16) for _ in range(3)])
        ps_tr = Ring([ph.ps([128, D], BF16) for _ in range(2)])
        ps_s = Ring([ph.ps([128, 512], F32) for _ in range(2)])
        ps_o = Ring([ph.ps([128, 512], F32) for _ in range(2)])
        vinit = []
        for b in vbuf.bufs:
            vinit.append(ph.op("pool", lambda e, b=b: e.memset(b[:], 1.0), sig=True))
        kinit = []
        for b in kbuf.bufs:
            kinit.append(ph.op("pool", lambda e, b=b: e.memset(b[:], 0.0), sig=True))
        last = []
        cnt = 0
        for u, S in enumerate(self.units):
            for g, dil in enumerate(DILS):
                l = S // dil
                ntile = l // 128
                cq, ck, cv = 0, D, 2 * D
                for r in range(dil):
                    base = self.uoff[u] + r
                    prev_q = None
                    for jk in range(ntile + 1):
                        i0 = 128 * jk - 64
                        lo, hi = max(i0, 0), min(i0 + 128, l)
                        p0 = lo - i0
                        nrow = hi - lo
                        half = nrow < 128
                        rows = slice(base + dil * lo, base + dil * (hi - 1) + 1, dil)
                        kk_, kb, kf = kbuf.get()
                        kv_, vb, vf = vbuf.get()
                        wz = []
                        if half:
                            z0 = 0 if p0 == 64 else 64
                            wz.append(ph.op("pool", lambda e, kb=kb, z0=z0: e.memset(kb[z0:z0 + 64, :], 0.0),
                                            waits=[kf] + kinit, sig=True))
                            wz.append(ph.op("pool", lambda e, vb=vb, z0=z0: e.memset(vb[z0:z0 + 64, :, 0:128], 0.0),
                                            waits=[vf] + vinit, sig=True))
                        ev_k = ph.op("sync", lambda e, kb=kb, rows=rows, p0=p0, nrow=nrow, ck=ck, g=g: e.dma_start(
                            out=kb[p0:p0 + nrow, :], in_=self.qkvg[g][rows, ck:ck + D]),
                            waits=[kf] + kinit + wz, sig=f"k{kk_}", dma=True)
                        ev_v = ph.op("sync", lambda e, vb=vb, rows=rows, p0=p0, nrow=nrow, cv=cv, g=g: e.dma_start(
                            out=vb[p0:p0 + nrow, :, 0:128],
                            in_=self.qkvg[g][rows, cv:cv + D].rearrange("t (h d) -> t h d", d=128)),
                            waits=[vf] + vinit + wz, sig=f"v{kv_}", dma=True)
                        kkt, ktb, ktf = kT.get()
                        ev_kt = self.emit_tr16(ph, kb, ktb, ident, ev_id, ps_tr, [ev_k, ktf])
                        kbuf.release(kk_, ev_kt["pe"])
                        cur_q = None
                        if jk < ntile:
                            qrows = slice(base + dil * 128 * jk, base + dil * (128 * jk + 127) + 1, dil)
                            kq_, qb, qf = qbuf.get()
                            ev_q = ph.op("sync", lambda e, qb=qb, qrows=qrows, cq=cq, g=g: e.dma_start(
                                out=qb[:], in_=self.qkvg[g][qrows, cq:cq + D]), waits=[qf], sig=f"q{kq_}", dma=True)
                            kqt, qtb, qtf = qT.get()
                            ev_qt = self.emit_tr16(ph, qb, qtb, ident, ev_id, ps_tr, [ev_q, qtf])
                            qbuf.release(kq_, ev_qt["pe"])
                            ka, ab, af = acc.get()
                            cur_q = dict(qtb=qtb, kqt=kqt, ev=ev_qt["evac"], ab=ab, ka=ka, af=af, qrows=qrows, uses=[], accev=[])
                        if jk == 0:
                            mtype = 1
                        elif jk == ntile:
                            mtype = 2
                        else:
                            mtype = 0
                        if ntile == 1 and False:
                            pass
                        parts = []
                        if prev_q is not None:
                            parts.append((0, prev_q, False))
                        if cur_q is not None:
                            parts.append((1, cur_q, True))
                        ev_pv_last = None
                        for h in range(NH):
                            slope = 2.0 ** (-8.0 * (g * NH + h + 1) / (NGRP * NH))
                            cgh = -slope * dil / scale
                            kps, pst_, psf = ps_s.get()
                            ev_s = None
                            for (col, qd, first) in parts:
                                ev_s = ph.op("pe", lambda e, pst_=pst_, ktb=ktb, qd=qd, h=h, col=col: e.matmul(
                                    pst_[:, col * 128:(col + 1) * 128], ktb[:, h, :], qd["qtb"][:, h, :], start=True, stop=True),
                                    waits=[ev_kt["evac"], qd["ev"], psf], sig=True)
                            c0 = parts[0][0] * 128
                            c1 = (parts[-1][0] + 1) * 128
                            kt_, tb, tf = tbuf.get()
                            ev_t = ph.op("dve", lambda e, tb=tb, pst_=pst_, cgh=cgh, mtype=mtype, c0=c0, c1=c1: e.scalar_tensor_tensor(
                                tb[:, c0:c1], absrel[:, mtype * 256 + c0:mtype * 256 + c1], cgh, pst_[:, c0:c1],
                                ALU.mult, ALU.add), waits=[ev_s, ev_ar, tf], sig=True)
                            ps_s.release(kps, ev_t)
                            kp_, pb, pf_ = pbuf.get()
                            ev_p = ph.op("act", lambda e, pb=pb, tb=tb, c0=c0, c1=c1: e.activation(
                                pb[:, c0:c1], tb[:, c0:c1], AF.Exp, scale=scale), waits=[ev_t, pf_], sig=True)
                            tbuf.release(kt_, ev_p)
                            kpo, pso, pof = ps_o.get()
                            ev_pv = None
                            for (col, qd, first) in parts:
                                ev_pv = ph.op("pe", lambda e, pso=pso, pb=pb, vb=vb, h=h, col=col: e.matmul(
                                    pso[:, col * 129:(col + 1) * 129], pb[:, col * 128:(col + 1) * 128], vb[:, h, :],
                                    start=True, stop=True), waits=[ev_p, ev_v, pof], sig=True)
                            pbuf.release(kp_, ev_pv)
                            ev_pv_last = ev_pv
                            ev_a = None
                            for (col, qd, first) in parts:
                                if first:
                                    ev_a = ph.op("dve", lambda e, qd=qd, pso=pso, h=h, col=col: e.tensor_copy(
                                        qd["ab"][:, h, :], pso[:, col * 129:(col + 1) * 129]),
                                        waits=[ev_pv, qd["af"]], sig=True)
                                else:
                                    ev_a = ph.op("dve", lambda e, qd=qd, pso=pso, h=h, col=col: e.tensor_tensor(
                                        qd["ab"][:, h, :], qd["ab"][:, h, :], pso[:, col * 129:(col + 1) * 129], ALU.add),
                                        waits=[ev_pv] + qd["accev"][h:h + 1], sig=True)
                                if first:
                                    qd["accev"].append(ev_a)
                                else:
                                    qd["accev"][h] = ev_a
                            ps_o.release(kpo, ev_a)
                        kT.release(kkt, ev_pv_last)
                        vbuf.release(kv_, ev_pv_last)
                        if prev_q is not None:
                            qd = prev_q
                            ev_st = ph.op("sync", lambda e, qd=qd, g=g: e.dma_start(
                                out=self.og[g][qd["qrows"], :], in_=qd["ab"][:].rearrange("p h d -> p (h d)")),
                                waits=qd["accev"], sig=f"o{qd['ka']}", dma=True)
                            acc.release(qd["ka"], ev_st)
                            qT.release(qd["kqt"], ev_pv_last)
                            last.append(ev_st)
                        prev_q = cur_q
        ph.op("sync", None, waits=last[-3:])
        ph.emit()

    def emit_tr16(self, ph, src, dst, ident, ev_id, ps_tr, waits):
        kp, pt, pf = ps_tr.get()
        ev_pe = None
        for c in range(NH):
            ev_pe = ph.op("pe", lambda e, pt=pt, c=c: e.transpose(pt[:, c * 128:(c + 1) * 128], src[:, c * 128:(c + 1) * 128], ident[:]),
                          waits=[ev_id, pf] + list(waits), sig=True)
        ev_ev = ph.op("act", lambda e, pt=pt: e.copy(dst[:].rearrange("p h t -> p (h t)"), pt[:]), waits=[ev_pe], sig=True)
        ps_tr.release(kp, ev_ev)
        return {"pe": ev_pe, "evac": ev_ev}

    def phase_proj_ln(self, kind, i, j):
        ph = self.ph(kind)
        alpha = (2.0 * self.depth_full) ** 0.25
        lnidx = 2 * i + (1 if kind == "mlp" else 0)
        if kind == "oproj":
            KCX, W2 = 16, self.w_o[j]
        elif kind == "gate":
            KCX, W2 = 32, self.w_out[j]
        else:
            KCX, W2 = 64, self.w2[i]
        ident, ev_id = self.load_const(ph, self.c_ident[:, :], [128, 128], BF16)
        gb = ph.sb([128, D], F32)
        bb = ph.sb([128, D], F32)
        ev_g = ph.op("sync", lambda e: e.dma_start(out=gb[:], in_=self.lng[lnidx:lnidx + 1, :].broadcast_to([128, D])), sig="cg", dma=True)
        ev_b = ph.op("sync", lambda e: e.dma_start(out=bb[:], in_=self.lnb[lnidx:lnidx + 1, :].broadcast_to([128, D])), sig="cb", dma=True)
        actT = Ring([ph.sb([128, KCX, 512], BF16) for _ in range(1)])
        w2s = Ring([ph.sb([128, 16, 512], BF16) for _ in range(2)])
        xold = Ring([ph.sb([128, 4, D], F32) for _ in range(1)])
        xnb = Ring([ph.sb([128, D], BF16) for _ in range(2)])
        stg = Ring([ph.sb([128, KC, 512], BF16) for _ in range(1)])
        stat = ph.sb([128, 4, 6], F32)
        mv = ph.sb([128, 2], F32)
        rstd = ph.sb([128, 1], F32)
        epsb = ph.sb([128, 1], F32)
        ev_eps = ph.op("pool", lambda e: e.memset(epsb[:], LN_EPS), sig=True)
        ps_acc = [ph.ps([128, 512], F32) for _ in range(4)]
        ps_accf = [None] * 4
        ps_tr = Ring([ph.ps([128, D], BF16) for _ in range(1)])
        ps_a = Ring([ph.ps([128, 512], F32) for _ in range(2)])
        if kind == "mlp":
            xt = Ring([ph.sb([128, KC, 512], BF16) for _ in range(1)])
            w1s = Ring([ph.sb([128, KC, 256], BF16) for _ in range(2)])
            rl = Ring([ph.sb([128, 512], F32) for _ in range(2)])
        elif kind == "oproj":
            ogb = [Ring([ph.sb([128, NH, 129], F32) for _ in range(1)]) for _ in range(3)]
            rec = ph.sb([128, NH], F32)
            ob = Ring([ph.sb([128, D], BF16) for _ in range(2)])
        else:
            yfb = Ring([ph.sb([128, DIN], F32) for _ in range(1)])
            ybb = Ring([ph.sb([128, DIN], F32) for _ in range(1)])
            zzb = Ring([ph.sb([128, DIN], F32) for _ in range(1)])
            xsb = Ring([ph.sb([128, DIN], BF16) for _ in range(1)])
            ub = Ring([ph.sb([128, DIN], BF16) for _ in range(1)])
            junk = ph.sb([128, 512], F32)
            ss = ph.sb([128, 8], F32)
            dskb, ev_dsk = self.load_const(ph, self.dsk[j], [128, 64], F32, eng="sync")
            nwb, ev_nw = self.load_const(ph, self.normw[j], [128, 32], F32, eng="sync")
        last = []
        serial = [None]
        for nt in range(self.NT):
            t512 = nt * 512
            ka, aT, aTf = actT.get()
            kxo, xo, xof = xold.get()
            ev_xo = ph.op("sync", lambda e, xo=xo, t512=t512: e.dma_start(
                out=xo[:], in_=self.xres[t512:t512 + 512, :].rearrange("(s p) n -> p s n", p=128)),
                waits=[xof], sig="xo", dma=True)
            aT_ready = []
            if kind == "mlp":
                kx, xb, xf = xt.get()
                ev_x = ph.op("sync", lambda e, xb=xb, nt=nt: e.dma_start(out=xb[:], in_=self.xT[nt]), waits=[xf], sig="x", dma=True)
                ev_mm = None
                for sl in range(DFF // 256):
                    kw, wb, wf = w1s.get()
                    ev_w = ph.op("pool", lambda e, wb=wb, sl=sl: e.dma_start(
                        out=wb[:], in_=self.w1[i][:, sl * 256:(sl + 1) * 256].rearrange("(c p) n -> p c n", p=128)),
                        waits=[wf], sig=f"w1{kw}", dma=True)
                    for cc in range(2):
                        fc = sl * 2 + cc
                        kp, pt, pf = ps_a.get()
                        for c in range(KC):
                            ev_mm = ph.op("pe", lambda e, pt=pt, wb=wb, xb=xb, c=c, cc=cc: e.matmul(
                                pt[:], wb[:, c, cc * 128:(cc + 1) * 128], xb[:, c, :], start=(c == 0), stop=(c == KC - 1)),
                                waits=[ev_w, ev_x, pf], sig=(True if c == KC - 1 else None))
                        kr, rb, rf = rl.get()
                        ev_r = ph.op("act", lambda e, rb=rb, pt=pt: e.activation(rb[:], pt[:], AF.Relu), waits=[ev_mm, rf], sig=True)
                        ps_a.release(kp, ev_r)
                        ev_h = ph.op("dve", lambda e, aT=aT, rb=rb, fc=fc: e.tensor_tensor(aT[:, fc, :], rb[:], rb[:], ALU.mult),
                                     waits=[ev_r, aTf], sig=True)
                        rl.release(kr, ev_h)
                        aT_ready.append(ev_h)
                    w1s.release(kw, ev_mm)
                xt.release(kx, ev_mm)
            elif kind == "oproj":
                for sub in range(4):
                    t0 = t512 + sub * 128
                    bufs = []
                    evl = []
                    for g in range(3):
                        kg, gbuf, gf = ogb[g].get()
                        evl.append(ph.op("sync", lambda e, gbuf=gbuf, g=g, t0=t0: e.dma_start(
                            out=gbuf[:].rearrange("p h d -> p (h d)"), in_=self.og[g][t0:t0 + 128, :]),
                            waits=[gf], sig=f"og{g}", dma=True))
                        bufs.append((kg, gbuf))
                    b0, b1, b2 = bufs[0][1], bufs[1][1], bufs[2][1]
                    e1 = ph.op("dve", lambda e, b0=b0, b1=b1: e.tensor_tensor(b0[:], b0[:], b1[:], ALU.add), waits=evl[:2], sig=True)
                    e2 = ph.op("dve", lambda e, b0=b0, b2=b2: e.tensor_tensor(b0[:], b0[:], b2[:], ALU.add), waits=[e1, evl[2]], sig=True)
                    e3 = ph.op("dve", lambda e, b0=b0: e.reciprocal(rec[:], b0[:, :, 128]), waits=[e2, serial[0]], sig=True)
                    ko, obuf, of_ = ob.get()
                    e4 = ph.op("dve", lambda e, b0=b0, obuf=obuf: e.tensor_tensor(
                        obuf[:].rearrange("p (h d) -> p h d", d=128), b0[:, :, 0:128],
                        bcl(rec[:], 128), ALU.mult), waits=[e3, of_], sig=True)
                    serial[0] = e4
                    ogb[0].release(bufs[0][0], e4)
                    ogb[1].release(bufs[1][0], e1)
                    ogb[2].release(bufs[2][0], e2)
                    ev_t = self.emit_transposes(ph, obuf, 16, ident, ev_id, ps_tr, aT, sub, [e4, aTf])
                    ob.release(ko, ev_t["pe"])
                    aT_ready.append(ev_t["evac"])
            else:
                for sub in range(4):
                    t0 = t512 + sub * 128
                    k1, yf_, f1 = yfb.get()
                    k2, yb_, f2 = ybb.get()
                    k3, zz_, f3 = zzb.get()
                    k4, xs_, f4 = xsb.get()
                    l1 = ph.op("sync", lambda e, yf_=yf_, t0=t0: e.dma_start(out=yf_[:], in_=self.yfb[0][t0:t0 + 128, :]), waits=[f1], sig="l1", dma=True)
                    l2 = ph.op("sync", lambda e, yb_=yb_, t0=t0: e.dma_start(out=yb_[:], in_=self.yfb[1][t0:t0 + 128, :]), waits=[f2], sig="l2", dma=True)
                    l3 = ph.op("sync", lambda e, zz_=zz_, t0=t0: e.dma_start(out=zz_[:], in_=self.zz[t0:t0 + 128, :]), waits=[f3], sig="l3", dma=True)
                    l4 = ph.op("sync", lambda e, xs_=xs_, t0=t0: e.dma_start(out=xs_[:], in_=self.xs[t0:t0 + 128, :]), waits=[f4], sig="l4", dma=True)
                    e1 = ph.op("dve", lambda e, yf_=yf_, yb_=yb_: e.tensor_tensor(yf_[:], yf_[:], yb_[:], ALU.add), waits=[l1, l2], sig=True)
                    e2 = ph.op("dve", lambda e, yb_=yb_, xs_=xs_: e.tensor_tensor(
                        yb_[:].rearrange("p (h q) -> p h q", q=64), xs_[:].rearrange("p (h q) -> p h q", q=64),
                        bcl(dskb[:], 64), ALU.mult), waits=[e1, l4, ev_dsk], sig=True)
                    e3 = ph.op("dve", lambda e, yf_=yf_, yb_=yb_: e.tensor_tensor(yf_[:], yf_[:], yb_[:], ALU.add), waits=[e2], sig=True)
                    e4 = ph.op("act", lambda e, zz_=zz_: e.activation(zz_[:], zz_[:], AF.Silu), waits=[l3], sig=True)
                    e5 = ph.op("dve", lambda e, yf_=yf_, zz_=zz_: e.tensor_tensor(yf_[:], yf_[:], zz_[:], ALU.mult), waits=[e3, e4], sig=True)
                    e6 = None
                    for gg in range(8):
                        e6 = ph.op("act", lambda e, yf_=yf_, gg=gg: e.activation(
                            junk[:], yf_[:, gg * 512:(gg + 1) * 512], AF.Square, accum_out=ss[:, gg:gg + 1]),
                            waits=[e5, serial[0]], sig=True)
                    e7 = ph.op("dve", lambda e: e.tensor_scalar(ss[:], ss[:], 1.0 / 512, LN_EPS, ALU.mult, ALU.add), waits=[e6], sig=True)
                    e8a = ph.op("act", lambda e: e.activation(ss[:], ss[:], AF.Ln), waits=[e7], sig=True)
                    e8 = ph.op("act", lambda e: e.activation(ss[:], ss[:], AF.Exp, scale=-0.5), waits=[e8a], sig=True)
                    ku, ub_, uf = ub.get()
                    e9 = ph.op("dve", lambda e, yf_=yf_, ub_=ub_: e.tensor_tensor(
                        ub_[:].rearrange("p (g q) -> p g q", q=512), yf_[:].rearrange("p (g q) -> p g q", q=512),
                        bcl(ss[:], 512), ALU.mult), waits=[e8, uf], sig=True)
                    serial[0] = e9
                    e10 = e9
                    yfb.release(k1, e10)
                    ybb.release(k2, e3)
                    zzb.release(k3, e5)
                    xsb.release(k4, e2)
                    ev_t = self.emit_transposes(ph, ub_, 32, ident, ev_id, ps_tr, aT, sub, [e10, aTf], scale=nwb, scale_ev=ev_nw)
                    ub.release(ku, ev_t["pe"])
                    aT_ready.append(ev_t["evac"])
            ks_, sbuf, sfree = stg.get()
            tr_evs = []
            zevs = {}
            zbufs = {}
            nq = KCX // 16
            ev_mm = None
            for nsl in range(4):
                for q in range(nq):
                    kw, wb, wf = w2s.get()
                    ev_w = ph.op("pool", lambda e, wb=wb, q=q, nsl=nsl: e.dma_start(
                        out=wb[:], in_=W2[q * 2048:(q + 1) * 2048, nsl * 512:(nsl + 1) * 512].rearrange("(c p) n -> p c n", p=128)),
                        waits=[wf], sig=f"w2{kw}", dma=True)
                    for sub in range(4):
                        for c in range(16):
                            fst = (q == 0 and c == 0)
                            lst = (q == nq - 1 and c == 15)
                            ev_mm = ph.op("pe", lambda e, sub=sub, aT=aT, wb=wb, q=q, c=c, fst=fst, lst=lst: e.matmul(
                                ps_acc[sub][:], aT[:, q * 16 + c, sub * 128:(sub + 1) * 128], wb[:, c, :], start=fst, stop=lst),
                                waits=[ev_w, ps_accf[sub]] + (aT_ready if (nsl == 0 and q == 0 and sub == 0 and c == 0) else []),
                                sig=(True if lst else None))
                        if q == nq - 1:
                            zbufs[sub] = (0, xo[:, sub, :], None)
                            ev_z = ph.op("dve", lambda e, xo=xo, sub=sub, nsl=nsl: e.scalar_tensor_tensor(
                                xo[:, sub, nsl * 512:(nsl + 1) * 512], xo[:, sub, nsl * 512:(nsl + 1) * 512], alpha, ps_acc[sub][:],
                                ALU.mult, ALU.add), waits=[ev_mm, ev_xo], sig=True)
                            ps_accf[sub] = ev_z
                            zevs.setdefault(sub, []).append(ev_z)
                    if q < nq - 1:
                        kpd, ptd, pfd = ps_a.get()
                        ev_mk = ph.op("pe", lambda e, ptd=ptd: e.matmul(ptd[:, 0:8], ident[:], ident[:, 0:8], start=True, stop=True),
                                      waits=[pfd, ev_id], sig=True)
                        ps_a.release(kpd, ev_mk)
                        w2s.release(kw, ev_mk)
                    else:
                        w2s.release(kw, ev_mm)
            actT.release(ka, ev_mm)
            xo_done = []
            for sub in range(4):
                kz, zbuf, zf = zbufs[sub]
                t0 = t512 + sub * 128
                e_s = None
                for cch in range(4):
                    e_s = ph.op("dve", lambda e, zbuf=zbuf, cch=cch: e.bn_stats(stat[:, cch, :], zbuf[:, cch * 512:(cch + 1) * 512]),
                                waits=zevs[sub] + [serial[0]], sig=True)
                e_a = ph.op("dve", lambda e: e.bn_aggr(mv[:], stat[:].rearrange("p a b -> p (a b)")), waits=[e_s], sig=True)
                e_r0 = ph.op("act", lambda e: e.activation(rstd[:], mv[:, 1:2], AF.Ln, bias=epsb[:, 0:1]), waits=[e_a, ev_eps], sig=True)
                e_r = ph.op("act", lambda e: e.activation(rstd[:], rstd[:], AF.Exp, scale=-0.5), waits=[e_r0], sig=True)
                e_n = ph.op("dve", lambda e, zbuf=zbuf: e.tensor_scalar(zbuf, zbuf, mv[:, 0:1], rstd[:, 0:1], ALU.subtract, ALU.mult),
                            waits=[e_r], sig=True)
                serial[0] = e_n
                e_g = ph.op("pool", lambda e, zbuf=zbuf: e.tensor_tensor(zbuf, zbuf, gb[:], ALU.mult), waits=[e_n, ev_g], sig=True)
                e_b = ph.op("pool", lambda e, zbuf=zbuf: e.tensor_tensor(zbuf, zbuf, bb[:], ALU.add), waits=[e_g, ev_b], sig=True)
                ev_st = ph.op("sync", lambda e, zbuf=zbuf, t0=t0: e.dma_start(out=self.xres[t0:t0 + 128, :], in_=zbuf),
                              waits=[e_b], sig=f"sx{sub}", dma=True)
                last.append(ev_st)
                kn, nb_, nf = xnb.get()
                e_c = ph.op("act", lambda e, nb_=nb_, zbuf=zbuf: e.copy(nb_[:], zbuf), waits=[e_b, nf], sig=True)
                xo_done += [ev_st, e_c]
                ev_t = self.emit_transposes(ph, nb_, KC, ident, ev_id, ps_tr, sbuf, sub, [e_c, sfree])
                xnb.release(kn, ev_t["pe"])
                tr_evs.append(ev_t["evac"])
            xold.release(kxo, xo_done)
            ev_st = ph.op("sync", lambda e, sbuf=sbuf, nt=nt: e.dma_start(out=self.xT[nt], in_=sbuf[:]), waits=tr_evs, sig="sT", dma=True)
            stg.release(ks_, ev_st)
            last.append(ev_st)
        ph.op("sync", None, waits=last[-6:])
        ph.emit()

    def phase_inproj(self, j):
        ph = self.ph("inproj")
        W = self.w_in[j]
        xt = Ring([ph.sb([128, KC, 512], BF16) for _ in range(2)])
        wsl = Ring([ph.sb([128, KC, 512], BF16) for _ in range(4)])
        stgz = Ring([ph.sb([128, 4, 512], F32) for _ in range(2)])
        stgx = Ring([ph.sb([128, 4, 512], BF16) for _ in range(2)])
        pss = Ring([ph.ps([128, 512], F32) for _ in range(4)])
        last = []
        tile_unit = []
        for u, S in enumerate(self.units):
            for k in range(S // 512):
                tile_unit.append((u, k))
        for nt in range(self.NT):
            u, kt = tile_unit[nt]
            kx, xb, xf = xt.get()
            ev_x = ph.op("sync", lambda e, xb=xb, nt=nt: e.dma_start(out=xb[:], in_=self.xT[nt]), waits=[xf], sig=f"x{kx}", dma=True)
            ev_mm = None
            slabs = [(s * 512, 512, "z") for s in range(8)] + [(DIN + CONVD, 128, "dt")]
            for (c0, wdt, kind) in slabs:
                kw, wb, wf = wsl.get()
                ev_w = ph.op("pool", lambda e, wb=wb, c0=c0, wdt=wdt: e.dma_start(
                    out=wb[:, :, 0:wdt], in_=W[:, c0:c0 + wdt].rearrange("(c p) n -> p c n", p=128)), waits=[wf], sig=f"w{kw}", dma=True)
                ksg, sg, sgf = stgz.get()
                evs = []
                for sub in range(4):
                    kp, pt, pf = pss.get()
                    for c in range(KC):
                        ev_mm = ph.op("pe", lambda e, pt=pt, xb=xb, wb=wb, c=c, sub=sub, wdt=wdt: e.matmul(
                            pt[:, 0:wdt], xb[:, c, sub * 128:(sub + 1) * 128], wb[:, c, 0:wdt], start=(c == 0), stop=(c == KC - 1)),
                            waits=[ev_x, ev_w, pf], sig=(True if c == KC - 1 else None))
                    ev_e = ph.op("act", lambda e, pt=pt, sg=sg, sub=sub, wdt=wdt: e.copy(sg[:, sub, 0:wdt], pt[:, 0:wdt]),
                                 waits=[ev_mm, sgf], sig=True)
                    pss.release(kp, ev_e)
                    evs.append(ev_e)
                wsl.release(kw, ev_mm)
                if kind == "z":
                    dst = self.zz[nt * 512:(nt + 1) * 512, c0:c0 + 512].rearrange("(s p) n -> p s n", p=128)
                else:
                    dst = self.dtr[nt * 512:(nt + 1) * 512, :].rearrange("(s p) n -> p s n", p=128)
                ev_st = ph.op("sync", lambda e, sg=sg, dst=dst, wdt=wdt: e.dma_start(out=dst, in_=sg[:, :, 0:wdt]),
                              waits=evs, sig=f"sz{ksg}", dma=True)
                stgz.release(ksg, ev_st)
                last.append(ev_st)
            for s in range(12):
                c0 = DIN + s * 512
                kw, wb, wf = wsl.get()
                ev_w = ph.op("pool", lambda e, wb=wb, c0=c0: e.dma_start(
                    out=wb[:], in_=W[:, c0:c0 + 512].rearrange("(c p) n -> p c n", p=128)), waits=[wf], sig=f"w{kw}", dma=True)
                ksg, sg, sgf = stgx.get()
                evs = []
                for cc in range(4):
                    kp, pt, pf = pss.get()
                    for c in range(KC):
                        ev_mm = ph.op("pe", lambda e, pt=pt, xb=xb, wb=wb, c=c, cc=cc: e.matmul(
                            pt[:], wb[:, c, cc * 128:(cc + 1) * 128], xb[:, c, :], start=(c == 0), stop=(c == KC - 1)),
                            waits=[ev_x, ev_w, pf], sig=(True if c == KC - 1 else None))
                    ev_e = ph.op("dve", lambda e, pt=pt, sg=sg, cc=cc: e.tensor_copy(sg[:, cc, :], pt[:]), waits=[ev_mm, sgf], sig=True)
                    pss.release(kp, ev_e)
                    evs.append(ev_e)
                wsl.release(kw, ev_mm)
                ev_st = ph.op("sync", lambda e, sg=sg, u=u, s=s, kt=kt: e.dma_start(
                    out=self.xbcT[u][s * 4:(s + 1) * 4, :, 2 + kt * 512:2 + (kt + 1) * 512].rearrange("c p t -> p c t"), in_=sg[:]),
                    waits=evs, sig=f"sx{ksg}", dma=True)
                stgx.release(ksg, ev_st)
                last.append(ev_st)
            xt.release(kx, ev_mm)
        ph.op("sync", None, waits=last[-4:])
        ph.emit()

    def phase_conv(self, j):
        ph = self.ph("conv")
        identb, ev_id = self.load_const(ph, self.c_ident[:, :], [128, 128], BF16)
        cw, ev_cw = self.load_const(ph, self.convw[j], [128, 240], F32, eng="sync")
        cb, ev_cb = self.load_const(ph, self.convb[j], [128, 48], F32, eng="sync")
        diag = ph.sb([128, 240, 128], BF16)
        zt = ph.sb([128, 48, 2], BF16)
        ev_z = ph.op("pool", lambda e: e.memset(zt[:], 0.0), sig=True)
        hal = []
        for u, S in enumerate(self.units):
            hal.append(ph.op("sync", lambda e, u=u: e.dma_start(out=self.xbcT[u][:, :, 0:2].rearrange("c p t -> p c t"), in_=zt[:]),
                             waits=[ev_z], sig="h0", dma=True))
            hal.append(ph.op("sync", lambda e, u=u, S=S: e.dma_start(out=self.xbcT[u][:, :, S + 2:S + 4].rearrange("c p t -> p c t"), in_=zt[:]),
                             waits=[ev_z], sig="h0", dma=True))
        ev_d = None
        for m in range(240):
            ev_d = ph.op("dve", lambda e, m=m: e.tensor_scalar(diag[:, m, :], identb[:], cw[:, m:m + 1], None, ALU.mult),
                         waits=[ev_id, ev_cw], sig=True)
        ub = Ring([ph.sb([128, 48, 516], BF16) for _ in range(2)])
        cv = Ring([ph.sb([128, 512], BF16) for _ in range(3)])
        tm = Ring([ph.sb([128, 4, 40 * 128], BF16) for _ in range(1)])
        psc = Ring([ph.ps([128, 512], F32) for _ in range(3)])
        pst = Ring([ph.ps([128, 512], BF16) for _ in range(2)])
        last = []
        for u, S in enumerate(self.units):
            for kt in range(S // 512):
                t512 = self.uoff[u] + kt * 512
                ku, ubuf, uf = ub.get()
                ev_u = ph.op("sync", lambda e, ubuf=ubuf, u=u, kt=kt: e.dma_start(
                    out=ubuf[:], in_=self.xbcT[u][:, :, kt * 512:kt * 512 + 516].rearrange("c p t -> p c t")),
                    waits=[uf] + hal, sig=f"u{ku}", dma=True)
                ktm, tmb, tmf = tm.get()
                tm_evs = []
                ev_mm = None
                for ct in range(48):
                    kp, pt, pf = psc.get()
                    for k in range(5):
                        ev_mm = ph.op("pe", lambda e, pt=pt, ubuf=ubuf, ct=ct, k=k: e.matmul(
                            pt[:], diag[:, ct * 5 + k, :], ubuf[:, ct, k:k + 512], start=(k == 0), stop=(k == 4)),
                            waits=[ev_u, ev_d, pf], sig=(True if k == 4 else None))
                    kc_, cvb, cvf = cv.get()
                    ev_a = ph.op("act", lambda e, cvb=cvb, pt=pt, ct=ct: e.activation(cvb[:], pt[:], AF.Silu, bias=cb[:, ct:ct + 1]),
                                 waits=[ev_mm, ev_cb, cvf], sig=True)
                    psc.release(kp, ev_a)
                    rel = []
                    if ct >= 32:
                        ev_s = ph.op("sync", lambda e, cvb=cvb, ct=ct, t512=t512: e.dma_start(
                            out=self.bct[ct - 32, :, t512:t512 + 512], in_=cvb[:]), waits=[ev_a], sig=f"sb{kc_}", dma=True)
                        rel.append(ev_s)
                        last.append(ev_s)
                    if ct < 40:
                        kq, ptt, ptf = pst.get()
                        ev_t = None
                        for sub in range(4):
                            ev_t = ph.op("pe", lambda e, ptt=ptt, cvb=cvb, sub=sub: e.transpose(
                                ptt[:, sub * 128:(sub + 1) * 128], cvb[:, sub * 128:(sub + 1) * 128], identb[:]),
                                waits=[ev_a, ptf, ev_id], sig=True)
                        ev_e = ph.op("dve", lambda e, ptt=ptt, tmb=tmb, ct=ct: e.tensor_copy(
                            tmb[:, :, ct * 128:(ct + 1) * 128], ptt[:].rearrange("p (s c) -> p s c", c=128)),
                            waits=[ev_t, tmf], sig=True)
                        pst.release(kq, ev_e)
                        rel.append(ev_t)
                        tm_evs.append(ev_e)
                    cv.release(kc_, rel)
                ub.release(ku, ev_mm)
                e1 = ph.op("sync", lambda e, tmb=tmb, t512=t512: e.dma_start(
                    out=self.xs[t512:t512 + 512, :].rearrange("(s p) c -> p s c", p=128), in_=tmb[:, :, 0:DIN]),
                    waits=tm_evs, sig="sxs", dma=True)
                e2 = ph.op("sync", lambda e, tmb=tmb, t512=t512: e.dma_start(
                    out=self.btm[t512:t512 + 512, :].rearrange("(s p) c -> p s c", p=128), in_=tmb[:, :, DIN:DIN + 1024]),
                    waits=tm_evs, sig="sbt", dma=True)
                tm.release(ktm, [e1, e2])
                last += [e1, e2]
        ph.op("sync", None, waits=last[-24:])
        ph.emit()

    def phase_scan(self, j):
        ph = self.ph("scan")
        identf, ev_idf = self.load_const(ph, self.c_ident[:, :], [128, 128], F32, eng="sync")
        identb, ev_idb = self.load_const(ph, self.c_ident[:, :], [128, 128], BF16)
        tri, ev_tri = self.load_const(ph, self.c_tri[:, :], [128, 256], F32, eng="sync")
        ones, ev_ones = self.load_const(ph, self.c_ones[:, :], [128, 128], F32, eng="sync")
        negm, ev_negm = self.load_const(ph, self.c_negm[:, :], [128, 1024], BF16)
        sel, ev_sel = self.load_const(ph, self.c_sel[:, :], [64, 64 * 128], F32, eng="sync")
        dtbb, ev_dtb = self.load_const(ph, self.dtb[j], [128, 128], F32, eng="sync")
        alb, ev_al = self.load_const(ph, self.alog[j], [128, 128], F32, eng="sync")
        abc = ph.sb([128, 128], F32)
        e0 = ph.op("act", lambda e: e.activation(abc[:], alb[:], AF.Exp), waits=[ev_al], sig=True)
        ev_a = ph.op("dve", lambda e: e.tensor_scalar(abc[:], abc[:], -1.0, None, ALU.mult), waits=[e0], sig=True)
        cdeps = [ev_idf, ev_idb, ev_tri, ev_ones, ev_negm, ev_sel, ev_dtb, ev_a]
        state = ph.sb([128, DIN], F32)
        state_bf = ph.sb([128, DIN], BF16)
        xsb = Ring([ph.sb([128, DIN], BF16) for _ in range(2)])
        btb = Ring([ph.sb([128, 1024], BF16) for _ in range(2)])
        BTb = Ring([ph.sb([128, 8, 128], BF16) for _ in range(2)])
        CTb = Ring([ph.sb([128, 8, 128], BF16) for _ in range(2)])
        dtrb = Ring([ph.sb([128, 128], F32) for _ in range(2)])
        ybuf = Ring([ph.sb([128, DIN], F32) for _ in range(2)])
        xdt = ph.sb([128, DIN], BF16)
        xw = ph.sb([128, DIN], BF16)
        cbm = ph.sb([128, 8, 128], F32)
        sm = {n: ph.sb([128, 64], F32) for n in ("x", "ax", "ex", "ln", "dt", "la", "cs", "ncs", "E", "w", "tot", "cd")}
        csT = ph.sb([64, 128], F32)
        Efull = ph.sb([128, DIN], F32)
        cdfull = ph.sb([128, DIN], F32)
        ncsT = ph.sb([64, 128], F32)
        Lb = Ring([ph.sb([128, 128], F32) for _ in range(3)])
        Mb = Ring([ph.sb([128, 128], BF16) for _ in range(3)])
        tmpo = Ring([ph.sb([128, 512], F32) for _ in range(2)])
        ps_small = Ring([ph.ps([128, 512], F32) for _ in range(1)])
        ps_arg = Ring([ph.ps([128, 512], F32) for _ in range(2)])
        ps_cb = Ring([ph.ps([128, 512], F32) for _ in range(1)])
        ps_y = Ring([ph.ps([128, 512], F32) for _ in range(2)])
        ps_o = Ring([ph.ps([128, 512], F32) for _ in range(1)])
        ps_st = Ring([ph.ps([128, 512], F32) for _ in range(1)])
        last = []
        prev_chunk_done = [None]
        for u, S in enumerate(self.units):
            nchunk = S // 128
            import os as _os
            for d in range(int(_os.environ.get("SCAN_DIRS", "2"))):
                ev_s0 = ph.op("pool", lambda e: e.memset(state[:], 0.0), waits=[prev_chunk_done[0]], sig=True)
                ev_s1 = ph.op("pool", lambda e: e.memset(state_bf[:], 0.0), waits=[prev_chunk_done[0]], sig=True)
                st_ev = [ev_s0, ev_s1]
                order = range(nchunk) if d == 0 else range(nchunk - 1, -1, -1)
                import os as _os
                _mc = int(_os.environ.get("SCAN_MAXCHUNK", "0"))
                _sp = int(_os.environ.get("SCAN_PART", "0"))
                _ss = int(_os.environ.get("SCAN_SUB", "0"))
                if _mc:
                    order = list(order)[:_mc]
                for c in order:
                    t0 = self.uoff[u] + c * 128
                    k1, xs_, f1 = xsb.get()
                    k2, bt_, f2 = btb.get()
                    k3, BT_, f3 = BTb.get()
                    k4, CT_, f4 = CTb.get()
                    k5, dr_, f5 = dtrb.get()
                    l1 = ph.op("sync", lambda e, xs_=xs_, t0=t0: e.dma_start(out=xs_[:], in_=self.xs[t0:t0 + 128, :]), waits=[f1], sig=f"a{k1}", dma=True)
                    l2 = ph.op("sync", lambda e, bt_=bt_, t0=t0: e.dma_start(out=bt_[:], in_=self.btm[t0:t0 + 128, :]), waits=[f2], sig=f"b{k2}", dma=True)
                    l3 = ph.op("sync", lambda e, BT_=BT_, t0=t0: e.dma_start(out=BT_[:], in_=self.bct[0:8, :, t0:t0 + 128].rearrange("g n t -> n g t")), waits=[f3], sig=f"c{k3}", dma=True)
                    l4 = ph.op("sync", lambda e, CT_=CT_, t0=t0: e.dma_start(out=CT_[:], in_=self.bct[8:16, :, t0:t0 + 128].rearrange("g n t -> n g t")), waits=[f4], sig=f"d{k4}", dma=True)
                    l5 = ph.op("sync", lambda e, dr_=dr_, t0=t0: e.dma_start(out=dr_[:], in_=self.dtr[t0:t0 + 128, :]), waits=[f5], sig=f"e{k5}", dma=True)
                    dsl = slice(d * 64, (d + 1) * 64)
                    pc = prev_chunk_done[0]
                    a1 = ph.op("dve", lambda e, dr_=dr_, dsl=dsl: e.tensor_tensor(sm["x"][:], dr_[:, dsl], dtbb[:, dsl], ALU.add), waits=[l5, pc] + cdeps, sig=True)
                    a2 = ph.op("act", lambda e: e.activation(sm["ax"][:], sm["x"][:], AF.Abs), waits=[a1], sig=True)
                    a3 = ph.op("act", lambda e: e.activation(sm["ex"][:], sm["ax"][:], AF.Exp, scale=-1.0), waits=[a2], sig=True)
                    a4 = ph.op("act", lambda e: e.activation(sm["ln"][:], sm["ex"][:], AF.Ln, bias=ones[:, 0:1]), waits=[a3], sig=True)
                    a5 = ph.op("dve", lambda e: e.scalar_tensor_tensor(sm["dt"][:], sm["x"][:], 0.0, sm["ln"][:], ALU.max, ALU.add), waits=[a4], sig=True)
                    dtrb.release(k5, a1)
                    a6 = ph.op("dve", lambda e, dsl=dsl: e.tensor_tensor(sm["la"][:], sm["dt"][:], abc[:, dsl], ALU.mult), waits=[a5], sig=True)
                    if _sp == 1:
                        continue
                    kps, pss_, psf = ps_small.get()
                    m1 = ph.op("pe", lambda e, pss_=pss_, d=d: e.matmul(pss_[:, 0:64], tri[:, d * 128:(d + 1) * 128], sm["la"][:], start=True, stop=True),
                               waits=[a6, psf], sig=True)
                    m2 = ph.op("pe", lambda e, pss_=pss_: e.matmul(pss_[:, 64:128], ones[:], sm["la"][:], start=True, stop=True), waits=[a6], sig=True)
                    b1 = ph.op("dve", lambda e, pss_=pss_: e.tensor_copy(sm["cs"][:], pss_[:, 0:64]), waits=[m1, m2], sig=True)
                    b2 = ph.op("dve", lambda e, pss_=pss_: e.tensor_copy(sm["tot"][:], pss_[:, 64:128]), waits=[m2], sig=True)
                    b3 = ph.op("dve", lambda e: e.tensor_scalar(sm["ncs"][:], sm["cs"][:], -1.0, None, ALU.mult), waits=[b1], sig=True)
                    b4 = ph.op("act", lambda e: e.activation(sm["E"][:], sm["cs"][:], AF.Exp), waits=[b1], sig=True)
                    b5 = ph.op("dve", lambda e: e.tensor_tensor(sm["w"][:], sm["tot"][:], sm["cs"][:], ALU.subtract), waits=[b1, b2], sig=True)
                    b6 = ph.op("act", lambda e: e.activation(sm["w"][:], sm["w"][:], AF.Exp), waits=[b5], sig=True)
                    b7 = ph.op("act", lambda e: e.activation(sm["cd"][:], sm["tot"][:], AF.Exp), waits=[b2], sig=True)
                    b4f = ph.op("dve", lambda e: e.tensor_copy(Efull[:].rearrange("p (h q) -> p h q", q=64), bcl(sm["E"][:], 64)), waits=[b4, pc], sig=True)
                    b7f = ph.op("dve", lambda e: e.tensor_copy(cdfull[:].rearrange("p (h q) -> p h q", q=64), bcl(sm["cd"][:], 64)), waits=[b7, pc], sig=True)
                    m3 = ph.op("pe", lambda e, pss_=pss_: e.transpose(pss_[0:64, 128:256], sm["cs"][:], identf[:]), waits=[b1, b2], sig=True)
                    b8 = ph.op("dve", lambda e, pss_=pss_: e.tensor_copy(csT[:], pss_[0:64, 128:256]), waits=[m3], sig=True)
                    b9 = ph.op("dve", lambda e: e.tensor_scalar(ncsT[:], csT[:], -1.0, None, ALU.mult), waits=[b8], sig=True)
                    ps_small.release(kps, b8)
                    if _sp == 2:
                        continue
                    x1 = ph.op("dve", lambda e, xs_=xs_: e.tensor_tensor(
                        xdt[:].rearrange("p (h q) -> p h q", q=64), xs_[:].rearrange("p (h q) -> p h q", q=64),
                        bcl(sm["dt"][:], 64), ALU.mult), waits=[l1, a5, pc], sig=True)
                    x2 = ph.op("dve", lambda e: e.tensor_tensor(
                        xw[:].rearrange("p (h q) -> p h q", q=64), xdt[:].rearrange("p (h q) -> p h q", q=64),
                        bcl(sm["w"][:], 64), ALU.mult), waits=[x1, b6], sig=True)
                    xsb.release(k1, x1)
                    if _sp == 3:
                        continue
                    cb_evs = []
                    for g in range(8):
                        kcb, pcb, pcf = ps_cb.get()
                        m4 = ph.op("pe", lambda e, pcb=pcb, BT_=BT_, CT_=CT_, g=g: e.matmul(pcb[:, 0:128], BT_[:, g, :], CT_[:, g, :], start=True, stop=True),
                                   waits=[l3, l4, pcf], sig=True)
                        c1 = ph.op("dve", lambda e, pcb=pcb, g=g, d=d: e.tensor_tensor(cbm[:, g, :], pcb[:, 0:128], tri[:, d * 128:(d + 1) * 128], ALU.mult),
                                   waits=[m4, pc], sig=True)
                        ps_cb.release(kcb, c1)
                        cb_evs.append(c1)
                    BTb.release(k3, m4)
                    if _sp == 4:
                        continue
                    ky, yb_, yf_ = ybuf.get()
                    y_evs = []
                    ev_ymm_last = None
                    for g in range(8):
                        kpy, py, pyf = ps_y.get()
                        for hq in range(2):
                            kpa, pa, paf = ps_arg.get()
                            for hh in range(4):
                                h = g * 8 + hq * 4 + hh
                                n1 = ph.op("pe", lambda e, pa=pa, d=d, hh=hh: e.matmul(pa[:, hh * 128:(hh + 1) * 128], identb[:], negm[:, d * 512:d * 512 + 128], start=True, stop=False),
                                           waits=[paf] + cdeps, sig=None)
                                ph.op("pe", lambda e, pa=pa, h=h, hh=hh: e.matmul(
                                    pa[:, hh * 128:(hh + 1) * 128], sel[:, h * 128:(h + 1) * 128], csT[:], start=False, stop=False),
                                    waits=[b8])
                                n2 = ph.op("pe", lambda e, pa=pa, h=h, hh=hh: e.matmul(
                                    pa[:, hh * 128:(hh + 1) * 128], ncsT[:], sel[:, h * 128:(h + 1) * 128], start=False, stop=True),
                                    waits=[b9], sig=(True if hh == 3 else None))
                            for hh in range(4):
                                if _ss == 9:
                                    continue
                                h = g * 8 + hq * 4 + hh
                                kl, lb, lf = Lb.get()
                                _n3 = _os.environ.get("SCAN_N3", "nobias")
                                if _n3 == "bias":
                                    n3 = ph.op("act", lambda e, lb=lb, pa=pa, h=h, hh=hh: e.activation(
                                        lb[:], pa[:, hh * 128:(hh + 1) * 128], AF.Exp, bias=sm["ncs"][:, h:h + 1]),
                                        waits=[n2, b3, lf], sig=True)
                                elif _n3 == "nobias":
                                    n3 = ph.op("act", lambda e, lb=lb, pa=pa, h=h, hh=hh: e.activation(
                                        lb[:], pa[:, hh * 128:(hh + 1) * 128], AF.Exp),
                                        waits=[n2, b3, lf], sig=True)
                                else:
                                    n3a = ph.op("dve", lambda e, lb=lb, pa=pa, h=h, hh=hh: e.tensor_scalar(
                                        lb[:], pa[:, hh * 128:(hh + 1) * 128], sm["cs"][:, h:h + 1], None, ALU.subtract),
                                        waits=[n2, b1, lf], sig=True)
                                    n3 = ph.op("act", lambda e, lb=lb: e.activation(lb[:], lb[:], AF.Exp), waits=[n3a], sig=True)
                                if _ss == 1:
                                    continue
                                km, mb, mf = Mb.get()
                                n4 = ph.op("dve", lambda e, mb=mb, lb=lb, g=g: e.tensor_tensor(mb[:], lb[:], cbm[:, g, :], ALU.mult),
                                           waits=[n3, cb_evs[g], mf], sig=True)
                                Lb.release(kl, n4)
                                if _ss == 2:
                                    continue
                                hi = hq * 4 + hh
                                n5 = ph.op("pe", lambda e, py=py, mb=mb, h=h, hi=hi: e.matmul(
                                    py[:, hi * 64:(hi + 1) * 64], mb[:], xdt[:, h * 64:(h + 1) * 64], start=True, stop=True),
                                    waits=[n4, x1, pyf], sig=True)
                                Mb.release(km, n5)
                                ev_ymm_last = n5
                            ps_arg.release(kpa, n3 if _ss != 9 else n2)
                        if _ss in (1, 2, 3, 9):
                            continue
                        kpo, po, pof = ps_o.get()
                        o1 = ph.op("pe", lambda e, po=po, CT_=CT_, g=g: e.matmul(po[:], CT_[:, g, :], state_bf[:, g * 512:(g + 1) * 512], start=True, stop=True),
                                   waits=[l4, pof] + st_ev, sig=True)
                        if _ss == 4:
                            continue
                        kt_, tb, tf = tmpo.get()
                        o2a = ph.op("act", lambda e, tb=tb, po=po: e.copy(tb[:], po[:]), waits=[o1, tf], sig=True)
                        ps_o.release(kpo, o2a)
                        o2b = ph.op("act", lambda e, yb_=yb_, py=py, g=g: e.copy(yb_[:, g * 512:(g + 1) * 512], py[:]),
                                    waits=[ev_ymm_last, yf_], sig=True)
                        ps_y.release(kpy, o2b)
                        o2 = ph.op("dve", lambda e, tb=tb, g=g: e.tensor_tensor(
                            tb[:], tb[:], Efull[:, g * 512:(g + 1) * 512], ALU.mult),
                            waits=[o2a, b4f], sig=True)
                        o3 = ph.op("dve", lambda e, yb_=yb_, tb=tb, g=g: e.tensor_tensor(
                            yb_[:, g * 512:(g + 1) * 512], yb_[:, g * 512:(g + 1) * 512], tb[:], ALU.add),
                            waits=[o2, o2b], sig=True)
                        tmpo.release(kt_, o3)
                        y_evs.append(o3)
                    if _sp == 5:
                        continue
                    CTb.release(k4, o1)
                    new_st = []
                    for g in range(8):
                        kst, pst_, pstf = ps_st.get()
                        s1 = ph.op("pe", lambda e, pst_=pst_, bt_=bt_, g=g: e.matmul(pst_[:], bt_[:, g * 128:(g + 1) * 128], xw[:, g * 512:(g + 1) * 512], start=True, stop=True),
                                   waits=[l2, x2, pstf], sig=True)
                        s2 = ph.op("dve", lambda e, g=g: e.tensor_tensor(
                            state[:, g * 512:(g + 1) * 512], state[:, g * 512:(g + 1) * 512],
                            cdfull[:, g * 512:(g + 1) * 512], ALU.mult),
                            waits=[b7f, y_evs[-1]] + st_ev, sig=True)
                        s3 = ph.op("dve", lambda e, pst_=pst_, g=g: e.tensor_tensor(
                            state[:, g * 512:(g + 1) * 512], state[:, g * 512:(g + 1) * 512], pst_[:], ALU.add), waits=[s1, s2], sig=True)
                        ps_st.release(kst, s3)
                        s4 = ph.op("act", lambda e, g=g: e.copy(state_bf[:, g * 512:(g + 1) * 512], state[:, g * 512:(g + 1) * 512]),
                                   waits=[s3, y_evs[-1]], sig=True)
                        new_st += [s3, s4]
                    btb.release(k2, s1)
                    st_ev = [new_st[-2], new_st[-1]]
                    ev_st = ph.op("sync", lambda e, yb_=yb_, d=d, t0=t0: e.dma_start(out=self.yfb[d][t0:t0 + 128, :], in_=yb_[:]),
                                  waits=y_evs, sig=f"sy{ky}", dma=True)
                    ybuf.release(ky, ev_st)
                    last.append(ev_st)
                    prev_chunk_done[0] = new_st[-1]
                    prev_chunk_done[0] = ph.op("dve", lambda e: e.tensor_copy(sm["x"][:, 0:1], sm["x"][:, 0:1]),
                                               waits=[new_st[-1], new_st[-2], ev_ymm_last, s1], sig=True)
        ph.op("sync", None, waits=last[-2:])
        ph.emit()

    def phase_final(self):
        ph = self.ph("fin")
        src = self.xres[:, :]
        dbg = getattr(self, "debug_out", None)
        if dbg is not None:
            src = dbg(self)
        dst = self.y_out[:, :]
        if isinstance(src, tuple):
            src, dst = src
        ev = ph.op("pool", lambda e: e.dma_start(out=dst, in_=src), sig="f", dma=True)
        ph.op("sync", None, waits=[ev])
        ph.emit()


UNITS = (8192, 2048)
DEPTH = 4
_CACHE = {}


def _get_nc(units, depth, stage_limit, depth_full, debug_out=None):
    key = (units, depth, stage_limit, depth_full, debug_out)
    if key not in _CACHE:
        b = Builder(list(units), depth, stage_limit)
        b.depth_full = depth_full
        b.debug_out = debug_out
        b.build()
        _CACHE[key] = b
    return _CACHE[key]


def run_units(xa_list, xb_list, w, units=UNITS, depth=DEPTH, stage_limit=None, depth_full=DEPTH, trace=False, ncores=8, debug_out=None):
    b = _get_nc(tuple(units), depth, stage_limit, depth_full, debug_out)
    consts = host_consts()
    ns = max(depth // 2, 1)
    f = lambda a: np.ascontiguousarray(a, dtype=np.float32)
    common = {
        "attn_w_qkv": f(w["attn_w_qkv"][: (depth + 1) // 2]),
        "attn_w_o": f(w["attn_w_o"][: (depth + 1) // 2]),
        "ssd_w_in": f(w["ssd_w_in"][:ns]),
        "ssd_w_out": f(w["ssd_w_out"][:ns]),
        "mlp_w1": f(w["mlp_w1"][:depth]),
        "mlp_w2": f(w["mlp_w2"][:depth]),
        "ln_g": f(w["ln_g"][:depth].reshape(-1, D)),
        "ln_b": f(w["ln_b"][:depth].reshape(-1, D)),
        "convw": f(np.stack([w["ssd_conv_w"][k].T.reshape(48, 128, 5).transpose(1, 0, 2).reshape(128, 240) for k in range(ns)])),
        "convb": f(np.stack([w["ssd_conv_b"][k].reshape(48, 128).T for k in range(ns)])),
        "dtb": f(np.stack([np.broadcast_to(w["ssd_dt_bias"][k].reshape(1, 128), (128, 128)) for k in range(ns)])),
        "alog": f(np.stack([np.broadcast_to(w["ssd_a_log"][k].reshape(1, 128), (128, 128)) for k in range(ns)])),
        "dsk": f(np.stack([np.broadcast_to(w["ssd_d"][k].reshape(1, 64), (128, 64)) for k in range(ns)])),
        "normw": f(np.stack([w["ssd_norm_w"][k].reshape(32, 128).T for k in range(ns)])),
        "c_ident": consts["ident"], "c_tri": consts["tri"], "c_ones": consts["ones"], "c_negm": consts["negm"],
        "c_sel": consts["sel"], "c_absrel": consts["absrel"],
    }
    in_maps = []
    for c in range(ncores):
        m = dict(common)
        m["x_in"] = np.ascontiguousarray(np.concatenate([xa_list[c], xb_list[c]], 0), dtype=np.float32)
        in_maps.append(m)
    res = run_bass_kernel_spmd(b.nc, in_maps, core_ids=list(range(ncores)), trace=trace)
    return [r["y_out"] for r in res.results], res


def kernel(x_prompt, x_sample, attn_w_qkv, attn_w_o, ssd_w_in, ssd_conv_w, ssd_conv_b, ssd_dt_bias,
           ssd_a_log, ssd_d, ssd_norm_w, ssd_w_out, mlp_w1, mlp_w2, ln_g, ln_b):
    w = dict(attn_w_qkv=attn_w_qkv, attn_w_o=attn_w_o, ssd_w_in=ssd_w_in, ssd_conv_w=ssd_conv_w, ssd_conv_b=ssd_conv_b,
             ssd_dt_bias=ssd_dt_bias, ssd_a_log=ssd_a_log, ssd_d=ssd_d, ssd_norm_w=ssd_norm_w, ssd_w_out=ssd_w_out,
             mlp_w1=mlp_w1, mlp_w2=mlp_w2, ln_g=ln_g, ln_b=ln_b)
    x_prompt = np.asarray(x_prompt, np.float32)
    x_sample = np.asarray(x_sample, np.float32)
    SA, SB = UNITS
    za = np.zeros((SA, D), np.float32)
    zb = np.zeros((SB, D), np.float32)
    xa = [x_sample[c] if c < 2 else za for c in range(8)]
    xb = [x_prompt[c - 2] if 2 <= c < 6 else zb for c in range(8)]
    outs, _ = run_units(xa[:6], xb[:6], w, ncores=6)
    y_sample = np.stack([outs[c][:SA] for c in range(2)], 0)
    y_prompt = np.stack([outs[c][SA:SA + SB] for c in range(2, 6)], 0)
    return (y_prompt, y_sample)
```
